# Optimizing a Trainium2 kernel written in Bass

```python
import math
import jax, jax.numpy as jnp
from jax import lax
import numpy as np

D_MODEL = 1024
BATCH = 1
SEQ = 16384
DEPTH = 2

GRID_W = 64
CTX_LEN = 256
HEAD_DIM = 64
ROPE_BASE = 10000.0
EPS = 1e-6
NEG_INF = -1e30
A_Q_HEADS = 8
A_KV_HEADS = 2
A_GROUP = A_Q_HEADS // A_KV_HEADS
A_WIDTH = A_Q_HEADS * HEAD_DIM
A_KV_WIDTH = A_KV_HEADS * HEAD_DIM
WINDOW = 128
BLOCK = 128
B_HEADS = 4
B_WIDTH = B_HEADS * 2 * HEAD_DIM
ATTN_WIDTH = A_WIDTH + B_WIDTH
ATTN_IN = A_WIDTH + 2 * A_KV_WIDTH + 3 * B_WIDTH + ATTN_WIDTH
RNN_WIDTH = 1280
RNN_BLOCKS = 16
RNN_BLOCK_DIM = RNN_WIDTH // RNN_BLOCKS
CONV_WIDTH = 4
CONV_PAD_LEFT = 2
RG_C = 8.0
N_ATTN_LAYERS = (DEPTH + 1) // 2
N_REC_LAYERS = DEPTH // 2

kernel_name = "hybrid_swa_diffattn_rglru_prefix_ctx"


def rmsnorm(x, g):
    x32 = x.astype(jnp.float32)
    y = x32 * lax.rsqrt(jnp.mean(x32 * x32, axis=-1, keepdims=True) + EPS)
    return (y * g.astype(jnp.float32)).astype(x.dtype)


def adaln(cv, w, b):
    m = jax.nn.silu(cv) @ w + b
    return jnp.split(m, 3, axis=-1)


def axial_rope_tables(n):
    rows = n // GRID_W
    row = jnp.repeat(jnp.arange(rows, dtype=jnp.float32), GRID_W)
    col = jnp.tile(jnp.arange(GRID_W, dtype=jnp.float32), rows)
    n_freq = HEAD_DIM // 4
    inv_freq = ROPE_BASE ** (-jnp.arange(n_freq, dtype=jnp.float32) / n_freq)
    ar = row[:, None] * inv_freq
    ac = col[:, None] * inv_freq
    ang = jnp.concatenate([ar, ar, ac, ac], axis=-1)
    return jnp.cos(ang), jnp.sin(ang)


def apply_rope(x, cos, sin):
    q = HEAD_DIM // 4
    x0, x1, x2, x3 = x[..., :q], x[..., q:2 * q], x[..., 2 * q:3 * q], x[..., 3 * q:]
    rot = jnp.concatenate([-x1, x0, -x3, x2], axis=-1)
    shape = (1, x.shape[1]) + (1,) * (x.ndim - 3) + (HEAD_DIM,)
    return x * cos.reshape(shape).astype(x.dtype) + rot * sin.reshape(shape).astype(x.dtype)


def split_attn(p):
    Bn, n = p.shape[:2]
    s0 = A_WIDTH
    s1 = s0 + A_KV_WIDTH
    s2 = s1 + A_KV_WIDTH
    s3 = s2 + B_WIDTH
    s4 = s3 + B_WIDTH
    s5 = s4 + B_WIDTH
    qa = p[..., :s0].reshape(Bn, n, A_KV_HEADS, A_GROUP, HEAD_DIM)
    ka = p[..., s0:s1].reshape(Bn, n, A_KV_HEADS, HEAD_DIM)
    va = p[..., s1:s2].reshape(Bn, n, A_KV_HEADS, HEAD_DIM)
    qb = p[..., s2:s3].reshape(Bn, n, B_HEADS, 2, HEAD_DIM)
    kb = p[..., s3:s4].reshape(Bn, n, B_HEADS, 2, HEAD_DIM)
    vb = p[..., s4:s5].reshape(Bn, n, B_HEADS, 2 * HEAD_DIM)
    gate = p[..., s5:]
    return qa, ka, va, qb[..., 0, :], qb[..., 1, :], kb[..., 0, :], kb[..., 1, :], vb, gate


def window_attention(q, k, v, kc, vc, sink):
    Bn, n = q.shape[:2]
    nb = n // BLOCK
    n_ctx = kc.shape[1]
    scale = HEAD_DIM ** -0.5
    qb = q.reshape(Bn, nb, BLOCK, A_KV_HEADS, A_GROUP, HEAD_DIM)

    def band(t):
        t = t.reshape(Bn, nb, BLOCK, A_KV_HEADS, HEAD_DIM)
        t = jnp.pad(t, ((0, 0), (1, 1), (0, 0), (0, 0), (0, 0)))
        return jnp.concatenate([t[:, :-2], t[:, 1:-1], t[:, 2:]], axis=2)

    kw, vw = band(k), band(v)
    s_w = jnp.einsum('bnqhgd,bnkhd->bnhgqk', qb, kw).astype(jnp.float32) * scale
    qi = jnp.arange(BLOCK)[:, None]
    kj = jnp.arange(3 * BLOCK)[None, :]
    in_band = jnp.abs(kj - BLOCK - qi) <= WINDOW
    kblk = jnp.arange(nb)[:, None, None] + kj[None] // BLOCK - 1
    mask = in_band[None] & (kblk >= 0) & (kblk < nb)
    s_w = jnp.where(mask[None, :, None, None], s_w, NEG_INF)
    s_c = jnp.einsum('bnqhgd,bchd->bnhgqc', qb, kc).astype(jnp.float32) * scale
    sk = jnp.broadcast_to(sink.astype(jnp.float32).reshape(1, 1, A_KV_HEADS, A_GROUP, 1, 1), s_w.shape[:-1] + (1,))
    p = jax.nn.softmax(jnp.concatenate([s_w, s_c, sk], axis=-1), axis=-1)
    pw = p[..., :3 * BLOCK].astype(v.dtype)
    pc = p[..., 3 * BLOCK:3 * BLOCK + n_ctx].astype(v.dtype)
    o = jnp.einsum('bnhgqk,bnkhd->bnqhgd', pw, vw) + jnp.einsum('bnhgqc,bchd->bnqhgd', pc, vc)
    return o.reshape(Bn, n, A_WIDTH)


def ctx_sink_attention(qc, kc, vc, sink):
    Bn, n = qc.shape[:2]
    scale = HEAD_DIM ** -0.5
    s = jnp.einsum('bqhgd,bkhd->bhgqk', qc, kc).astype(jnp.float32) * scale
    sk = jnp.broadcast_to(sink.astype(jnp.float32).reshape(1, A_KV_HEADS, A_GROUP, 1, 1), s.shape[:-1] + (1,))
    p = jax.nn.softmax(jnp.concatenate([s, sk], axis=-1), axis=-1)[..., :-1].astype(vc.dtype)
    o = jnp.einsum('bhgqk,bkhd->bqhgd', p, vc)
    return o.reshape(Bn, n, A_WIDTH)


def diff_attention_block(q1, q2, k1, k2, v, lam):
    scale = HEAD_DIM ** -0.5
    s1 = jnp.einsum('bqhd,bkhd->bhqk', q1, k1).astype(jnp.float32) * scale
    s2 = jnp.einsum('bqhd,bkhd->bhqk', q2, k2).astype(jnp.float32) * scale
    p = jax.nn.softmax(s1, axis=-1) - lam * jax.nn.softmax(s2, axis=-1)
    return jnp.einsum('bhqk,bkhe->bqhe', p.astype(v.dtype), v)


def diff_attention_latent(q1, q2, k1, k2, v, lam):
    Bn, n = q1.shape[:2]
    nb = n // BLOCK

    def blocks(t):
        return t.reshape(Bn, nb, BLOCK, B_HEADS, HEAD_DIM).transpose(1, 0, 2, 3, 4)

    o = lax.map(lambda qs: diff_attention_block(qs[0], qs[1], k1, k2, v, lam), (blocks(q1), blocks(q2)))
    return o.transpose(1, 0, 2, 3, 4).reshape(Bn, n, B_HEADS, 2 * HEAD_DIM)


def diff_out(o, g, lam_init):
    Bn, n = o.shape[:2]
    return (rmsnorm(o, g) * (1.0 - lam_init)).reshape(Bn, n, B_WIDTH)


def attn_mixer(hl, hc, w_in, w_out, sink, lam_q1, lam_k1, lam_q2, lam_k2, subln_g, lam_init, cos, sin, need_ctx):
    qa, ka, va, q1, q2, k1, k2, vb, gl = split_attn(hl @ w_in)
    qac, kac, vac, q1c, q2c, k1c, k2c, vbc, gc = split_attn(hc @ w_in)
    qa, ka, q1, q2, k1, k2 = [apply_rope(t, cos, sin) for t in (qa, ka, q1, q2, k1, k2)]
    lam = (jnp.exp(jnp.sum(lam_q1.astype(jnp.float32) * lam_k1.astype(jnp.float32)))
           - jnp.exp(jnp.sum(lam_q2.astype(jnp.float32) * lam_k2.astype(jnp.float32))) + lam_init)
    oa = window_attention(qa, ka, va, kac, vac, sink)
    ob = diff_attention_latent(q1, q2, jnp.concatenate([k1, k1c], axis=1), jnp.concatenate([k2, k2c], axis=1),
                               jnp.concatenate([vb, vbc], axis=1), lam)
    ob = diff_out(ob, subln_g, lam_init)
    out_l = (jnp.concatenate([oa, ob], axis=-1) * jax.nn.silu(gl)) @ w_out
    if not need_ctx:
        return out_l, None
    oac = ctx_sink_attention(qac, kac, vac, sink)
    obc = diff_out(diff_attention_block(q1c, q2c, k1c, k2c, vbc, lam), subln_g, lam_init)
    out_c = (jnp.concatenate([oac, obc], axis=-1) * jax.nn.silu(gc)) @ w_out
    return out_l, out_c


def dwconv(u, w, b):
    y = lax.conv_general_dilated(u, w[:, None, :], window_strides=(1,),
                                 padding=[(CONV_PAD_LEFT, CONV_WIDTH - 1 - CONV_PAD_LEFT)],
                                 dimension_numbers=('NWC', 'WIO', 'NWC'), feature_group_count=u.shape[-1])
    return y + b


def block_diag(u, w):
    Bn, n = u.shape[:2]
    ub = u.reshape(Bn, n, RNN_BLOCKS, RNN_BLOCK_DIM)
    return jnp.einsum('bnhi,hij->bnhj', ub, w).reshape(Bn, n, RNN_WIDTH)


def rglru_coeffs(u, wa, ba, wx, bx, lam):
    r = jax.nn.sigmoid((block_diag(u, wa) + ba).astype(jnp.float32))
    i = jax.nn.sigmoid((block_diag(u, wx) + bx).astype(jnp.float32))
    log_a = -RG_C * r * jax.nn.softplus(-lam.astype(jnp.float32))
    a = jnp.exp(log_a)
    b = jnp.sqrt(-jnp.expm1(2.0 * log_a)) * i * u.astype(jnp.float32)
    return a, b


def linear_scan(a, b, h0, reverse):
    def combine(e1, e2):
        a1, b1 = e1
        a2, b2 = e2
        return a1 * a2, a2 * b1 + b2
    A, Bc = lax.associative_scan(combine, (a, b), axis=1, reverse=reverse)
    return A * h0[:, None, :] + Bc


def rec_mixer(hl, hc, w_in, conv_w, conv_b, wa, ba, wx, bx, lam, w_out, need_ctx):
    pl = hl @ w_in
    pc = hc @ w_in
    xl, gl = pl[..., :RNN_WIDTH], pl[..., RNN_WIDTH:]
    xc, gc = pc[..., :RNN_WIDTH], pc[..., RNN_WIDTH:]
    ul = dwconv(xl, conv_w, conv_b)
    uc = dwconv(xc, conv_w, conv_b)
    h_zero = jnp.zeros((uc.shape[0], RNN_WIDTH), jnp.float32)
    yl = jnp.zeros(ul.shape, jnp.float32)
    yc = jnp.zeros(uc.shape, jnp.float32)
    for d, rev in enumerate((False, True)):
        ac, bc = rglru_coeffs(uc, wa[d], ba[d], wx[d], bx[d], lam[d])
        hc_seq = linear_scan(ac, bc, h_zero, rev)
        h0 = hc_seq[:, 0] if rev else hc_seq[:, -1]
        al, bl = rglru_coeffs(ul, wa[d], ba[d], wx[d], bx[d], lam[d])
        yl = yl + linear_scan(al, bl, h0, rev)
        yc = yc + hc_seq
    out_l = (yl.astype(hl.dtype) * jax.nn.silu(gl)) @ w_out
    if not need_ctx:
        return out_l, None
    out_c = (yc.astype(hc.dtype) * jax.nn.silu(gc)) @ w_out
    return out_l, out_c


def setup_inputs(seed: int = 0) -> dict:
    key = jax.random.key(seed)
    ks = jax.random.split(key, 26)
    nrm = jax.random.normal
    D = D_MODEL
    u = jax.random.uniform(ks[22], (N_REC_LAYERS, 2, RNN_WIDTH), minval=0.9, maxval=0.999)
    a = u ** (1.0 / RG_C)
    return {
        'x': nrm(ks[0], (BATCH, SEQ, D)),
        'c': nrm(ks[1], (BATCH, D)),
        'ctx': nrm(ks[2], (BATCH, CTX_LEN, D)),
        'c_ctx': nrm(ks[3], (D,)),
        'norm_g': 1.0 + 0.05 * nrm(ks[4], (DEPTH, D)),
        'ada_w': nrm(ks[5], (DEPTH, D, 3 * D)) * D ** -0.5,
        'ada_b': 0.02 * nrm(ks[6], (DEPTH, 3 * D)),
        'attn_w_in': nrm(ks[7], (N_ATTN_LAYERS, D, ATTN_IN)) * D ** -0.5,
        'attn_w_out': nrm(ks[8], (N_ATTN_LAYERS, ATTN_WIDTH, D)) * ATTN_WIDTH ** -0.5,
        'attn_sink': nrm(ks[9], (N_ATTN_LAYERS, A_Q_HEADS)),
        'lam_q1': 0.1 * nrm(ks[10], (N_ATTN_LAYERS, HEAD_DIM)),
        'lam_k1': 0.1 * nrm(ks[11], (N_ATTN_LAYERS, HEAD_DIM)),
        'lam_q2': 0.1 * nrm(ks[12], (N_ATTN_LAYERS, HEAD_DIM)),
        'lam_k2': 0.1 * nrm(ks[13], (N_ATTN_LAYERS, HEAD_DIM)),
        'subln_g': 1.0 + 0.05 * nrm(ks[14], (N_ATTN_LAYERS, 2 * HEAD_DIM)),
        'rec_w_in': nrm(ks[15], (N_REC_LAYERS, D, 2 * RNN_WIDTH)) * D ** -0.5,
        'rec_conv_w': nrm(ks[16], (N_REC_LAYERS, CONV_WIDTH, RNN_WIDTH)) * CONV_WIDTH ** -0.5,
        'rec_conv_b': 0.02 * nrm(ks[17], (N_REC_LAYERS, RNN_WIDTH)),
        'rec_wa': nrm(ks[18], (N_REC_LAYERS, 2, RNN_BLOCKS, RNN_BLOCK_DIM, RNN_BLOCK_DIM)) * RNN_BLOCK_DIM ** -0.5,
        'rec_ba': 0.02 * nrm(ks[19], (N_REC_LAYERS, 2, RNN_WIDTH)),
        'rec_wx': nrm(ks[20], (N_REC_LAYERS, 2, RNN_BLOCKS, RNN_BLOCK_DIM, RNN_BLOCK_DIM)) * RNN_BLOCK_DIM ** -0.5,
        'rec_bx': 0.02 * nrm(ks[21], (N_REC_LAYERS, 2, RNN_WIDTH)),
        'rec_lam': jnp.log(a) - jnp.log1p(-a),
        'rec_w_out': nrm(ks[23], (N_REC_LAYERS, RNN_WIDTH, D)) * RNN_WIDTH ** -0.5,
        'final_g': 1.0 + 0.05 * nrm(ks[24], (D,)),
    }


def reference(x, c, ctx, c_ctx, norm_g, ada_w, ada_b, attn_w_in, attn_w_out, attn_sink, lam_q1, lam_k1,
              lam_q2, lam_k2, subln_g, rec_w_in, rec_conv_w, rec_conv_b, rec_wa, rec_ba, rec_wx, rec_bx,
              rec_lam, rec_w_out, final_g):
    n = x.shape[1]
    cos, sin = axial_rope_tables(n)
    xl, xc = x, ctx
    for l in range(DEPTH):
        need_ctx = l < DEPTH - 1
        sh, sc, gt = adaln(c, ada_w[l], ada_b[l])
        shc, scc, gtc = adaln(c_ctx, ada_w[l], ada_b[l])
        hl = rmsnorm(xl, norm_g[l]) * (1.0 + sc[:, None, :]) + sh[:, None, :]
        hc = rmsnorm(xc, norm_g[l]) * (1.0 + scc) + shc
        j = l // 2
        if l % 2 == 0:
            lam_init = 0.8 - 0.6 * math.exp(-0.3 * l)
            out_l, out_c = attn_mixer(hl, hc, attn_w_in[j], attn_w_out[j], attn_sink[j], lam_q1[j], lam_k1[j],
                                      lam_q2[j], lam_k2[j], subln_g[j], lam_init, cos, sin, need_ctx)
        else:
            out_l, out_c = rec_mixer(hl, hc, rec_w_in[j], rec_conv_w[j], rec_conv_b[j], rec_wa[j], rec_ba[j],
                                     rec_wx[j], rec_bx[j], rec_lam[j], rec_w_out[j], need_ctx)
        xl = xl + gt[:, None, :] * out_l
        if need_ctx:
            xc = xc + gtc * out_c
    return rmsnorm(xl, final_g)
```

```python
import math
import numpy as np
import ml_dtypes
import concourse.bass as bass
import concourse.mybir as mybir
from concourse.bass_utils import run_bass_kernel_spmd

F32 = mybir.dt.float32
BF16 = mybir.dt.bfloat16
AF = mybir.ActivationFunctionType
ALU = mybir.AluOpType
AX = mybir.AxisListType

ENGS = ("pe", "act", "dve", "pool", "sp")
NDSEM = 8

NCORES = 8
D = 1024
SEQ = 16384
NT = SEQ // NCORES
NTILE = NT // 128
CTX = 256
EPS = 1e-6
RW = 1280
NB = 16
BD = 80


class Prog:
    def __init__(self, nc):
        self.nc = nc
        self.lists = {e: [] for e in ENGS}
        self.count = {}
        self.waited = {e: {} for e in ENGS}
        self.last_write = {}
        self.readers = {}
        self.dma_n = {e: 0 for e in ENGS}
        self.dma_events = {e: {} for e in ENGS}
        self.semh = {}
        self.sb_off = 0
        self.arena_bytes = 206 * 1024
        self.arena = nc.alloc_sbuf_tensor("arena", [128, self.arena_bytes], mybir.dt.uint8)
        self.psum = nc.alloc_psum_tensor("psum", [128, 4096], F32)

    def bank(self, b, n=1):
        return self.psum[:, b * 512:(b + n) * 512]

    def sb(self, name, shape, dtype, off=None):
        nbytes = int(np.prod(shape[1:])) * mybir.dt.size(dtype)
        if off is None:
            off = self.sb_off
            self.sb_off = (off + nbytes + 63) // 64 * 64
        assert off + nbytes <= self.arena_bytes, (name, off, nbytes)
        ap = self.arena[:, off:off + nbytes].bitcast(dtype)
        if len(shape) == 3:
            ap = ap.rearrange("p (a b) -> p a b", a=shape[1])
        elif len(shape) == 4:
            ap = ap.rearrange("p (a b c) -> p a b c", a=shape[1], b=shape[2])
        if shape[0] < 128:
            ap = ap[0:shape[0]]
        return ap

    def _need(self, eng, ev, waits):
        if ev is None:
            return
        key, val = ev
        if self.waited[eng].get(key, 0) >= val:
            return
        self.waited[eng][key] = val
        for w in waits:
            if w[0] == key:
                if w[1] < val:
                    w[1] = val
                return
        waits.append([key, val])

    def _deps(self, eng, reads, writes):
        waits = []
        for r in reads:
            self._need(eng, self.last_write.get(r), waits)
        for w in writes:
            self._need(eng, self.last_write.get(w), waits)
            for ev in self.readers.get(w, ()):
                self._need(eng, ev, waits)
        return waits

    def _commit(self, ev, reads, writes):
        for r in reads:
            self.readers.setdefault(r, []).append(ev)
        for w in writes:
            self.last_write[w] = ev
            self.readers[w] = []

    def op(self, eng, fn, reads=(), writes=()):
        waits = self._deps(eng, reads, writes)
        key = ("e", eng)
        self.count[key] = self.count.get(key, 0) + 1
        ev = (key, self.count[key])
        self.lists[eng].append((waits, fn, (key, 1)))
        self._commit(ev, reads, writes)
        return ev

    def dma(self, q, fn, reads=(), writes=()):
        i = self.dma_n[q]
        self.dma_n[q] = i + 1
        waits = self._deps(q, reads, writes)
        if i >= NDSEM:
            self._need(q, self.dma_events[q][i - NDSEM], waits)
        key = ("d", q, i % NDSEM)
        ev = (key, 16 * (i // NDSEM + 1))
        self.dma_events[q][i] = ev
        self.lists[q].append((waits, fn, (key, 16)))
        self._commit(ev, reads, writes)
        return ev

    def wait_all(self, eng):
        waits = []
        for key, cnt in list(self.count.items()):
            self._need(eng, (key, cnt), waits)
        for q in ENGS:
            n = self.dma_n[q]
            for i in range(max(0, n - NDSEM), n):
                self._need(eng, self.dma_events[q][i], waits)
        self.lists[eng].append((waits, None, None))

    def barrier(self):
        for e in ENGS:
            self.wait_all(e)

    def emit(self):
        nc = self.nc
        keys = set()
        for e in ENGS:
            for waits, fn, sig in self.lists[e]:
                for k, _ in waits:
                    keys.add(k)
                if sig is not None:
                    keys.add(sig[0])
        for k in sorted(keys, key=str):
            self.semh[k] = nc.alloc_semaphore("s_" + "_".join(str(x) for x in k))

        def run(eng_name):
            def body(engine):
                for waits, fn, sig in self.lists[eng_name]:
                    for k, v in waits:
                        engine.wait_ge(self.semh[k], v)
                    if fn is None:
                        continue
                    ins = fn(engine)
                    ins.then_inc(self.semh[sig[0]], sig[1])
            return body

        with nc.Block() as block:
            block.tensor(run("pe"))
            block.scalar(run("act"))
            block.vector(run("dve"))
            block.gpsimd(run("pool"))
            block.sync(run("sp"))

    def ld(self, out, in_, w, r=(), q="sp"):
        return self.dma(q, lambda e: e.dma_start(out=out, in_=in_), reads=r, writes=w)

    def cp(self, eng, out, in_, r, w):
        if eng == "act":
            return self.op("act", lambda e: e.activation(out=out, in_=in_, func=AF.Copy), r, w)
        return self.op(eng, lambda e: e.tensor_copy(out=out, in_=in_), r, w)

    def tt(self, eng, out, a, b, op, r, w):
        return self.op(eng, lambda e: e.tensor_tensor(out=out, in0=a, in1=b, op=op), r, w)

    def ts(self, eng, out, a, s1, s2, op0, op1, r, w):
        if s2 is None:
            return self.op(eng, lambda e: e.tensor_scalar(out=out, in0=a, scalar1=s1, scalar2=None, op0=op0), r, w)
        return self.op(eng, lambda e: e.tensor_scalar(out=out, in0=a, scalar1=s1, scalar2=s2, op0=op0, op1=op1), r, w)

    def stt(self, eng, out, a, s, b, op0, op1, r, w):
        return self.op(eng, lambda e: e.scalar_tensor_tensor(out=out, in0=a, scalar=s, in1=b, op0=op0, op1=op1), r, w)

    def actf(self, out, in_, func, r, w, bias=0.0, scale=1.0, accum=None):
        if accum is None:
            return self.op("act", lambda e: e.activation(out=out, in_=in_, func=func, bias=bias, scale=scale), r, w)
        return self.op("act", lambda e: e.activation(out=out, in_=in_, func=func, bias=bias, scale=scale,
                                                     accum_out=accum), r, w)

    def mm(self, items, r, w):
        def fn(e):
            ins = None
            for (o, l, rh, st, sp) in items:
                ins = e.matmul(o, lhsT=l, rhs=rh, start=st, stop=sp)
            return ins
        return self.op("pe", fn, r, w)

    def tr(self, items, r, w):
        def fn(e):
            ins = None
            for (o, i, idn) in items:
                ins = e.transpose(out=o, in_=i, identity=idn)
            return ins
        return self.op("pe", fn, r, w)


def _dram(nc, name, shape, dtype, kind):
    return nc.dram_tensor(name, list(shape), dtype, kind=kind).ap()


class Builder:
    def __init__(self, stage):
        self.stage = stage
        self.nc = bass.Bass("TRN2", target_bir_lowering=False)
        self.P = Prog(self.nc)
        self.inputs = []
        self.outputs = []

    def din(self, name, shape, dtype=F32):
        self.inputs.append(name)
        return _dram(self.nc, name, shape, dtype, "ExternalInput")

    def dout(self, name, shape, dtype=F32):
        self.outputs.append(name)
        return _dram(self.nc, name, shape, dtype, "ExternalOutput")

    def consts(self):
        P = self.P
        ident_d = self.din("ident", [128, 128])
        self.ident_f = P.sb("ident_f", [128, 128], F32)
        self.ident_b = P.sb("ident_b", [128, 128], BF16)
        self.ones_b = P.sb("ones_b", [128, 128], BF16)
        self.ones_f = P.sb("ones_f", [128, 128], F32)
        self.mhalf = P.sb("mhalf", [128, 1], F32)
        P.ld(self.ident_f, ident_d, w=["ident_f"])
        P.cp("dve", self.ident_b, self.ident_f, ["ident_f"], ["ident_b"])
        P.op("pool", lambda e: e.memset(self.ones_b, 1.0), writes=["ones_b"])
        P.op("pool", lambda e: e.memset(self.ones_f, 1.0), writes=["ones_f"])
        P.op("pool", lambda e: e.memset(self.mhalf, -0.5), writes=["mhalf"])
        self.blk_b = P.sb("blk_b", [128, 128], BF16)
        P.op("pool", lambda e: e.memset(self.blk_b, 1.0), writes=["blk_b"])
        P.op("pool", lambda e: e.memset(self.blk_b[0:64, 64:128], 0.0), writes=["blk_b"])
        P.op("pool", lambda e: e.memset(self.blk_b[64:128, 0:64], 0.0), writes=["blk_b"])
        flags_d = self.din("flags", [128, 4])
        self.flags = P.sb("flags", [128, 4], F32)
        P.ld(self.flags, flags_d, w=["flags"])
        self.vec = {}
        for nm in ("gs", "sh", "gt", "gsc", "shc", "gtc"):
            self.vec[nm] = P.sb("vec_" + nm, [128, 1024], F32)
        self.ZA = P.sb_off
        self.ZP = self.ZA + 66 * 1024
        self.ZB = self.ZP + 56 * 1024

    def adaln(self, l):
        P = self.P
        if not hasattr(self, "ada_w_d"):
            self.ada_w_d = self.din("ada_w", [2, 1024, 3072])
            self.ada_b_d = self.din("ada_b", [2, 3072])
            self.norm_g_d = self.din("norm_g", [2, 1024])
            self.ccT_d = self.din("ccT", [128, 16])
            self.sil = P.sb("sil", [128, 16], F32)
            self.silB = P.sb("silB", [128, 16, 128], F32)
            self.adaw = [P.sb("adaw%d" % i, [128, 8, 512], F32) for i in range(2)]
            self.adab = P.sb("adab", [128, 512], F32)
            self.gbc = P.sb("gbc", [128, 1024], F32)
            P.ld(self.sil, self.ccT_d, w=["sil"])
            tmp = P.sb("siltmp", [128, 16], F32)
            P.actf(tmp, self.sil, AF.Exp, ["sil"], ["siltmp"], scale=-1.0)
            P.ts("dve", tmp, tmp, 1.0, None, ALU.add, None, ["siltmp"], ["siltmp"])
            P.op("dve", lambda e: e.reciprocal(out=tmp, in_=tmp), ["siltmp"], ["siltmp"])
            P.tt("dve", self.sil, self.sil, tmp, ALU.mult, ["sil", "siltmp"], ["sil"])
            for j in range(16):
                P.cp("dve", self.silB[:, j, :], self.sil[:, j:j + 1].to_broadcast([128, 128]), ["sil"], [("silB", j)])
        vec = self.vec
        wv = self.ada_w_d[l].rearrange("(k p) n -> p k n", p=128)
        P.ld(self.gbc, self.norm_g_d[l:l + 1, :].to_broadcast([128, 1024]), w=["gbc"])
        order = [("sh", "shc"), ("sh", "shc"), ("gs", "gsc"), ("gs", "gsc"), ("gt", "gtc"), ("gt", "gtc")]
        for n in range(6):
            wb = self.adaw[n % 2]
            wreg = ("adaw", n % 2)
            P.ld(wb, wv[:, :, n * 512:(n + 1) * 512], w=[wreg])
            P.ld(self.adab, self.ada_b_d[l:l + 1, n * 512:(n + 1) * 512].to_broadcast([128, 512]), w=["adab"])
            for v in range(2):
                items = [(P.bank(v), self.silB[:, 2 * k + v, :], wb[:, k, :], k == 0, k == 7) for k in range(8)]
                P.mm(items, [wreg] + [("silB", 2 * k + v) for k in range(8)], [("ps", v)])
                dst = vec[order[n][v]][:, (n % 2) * 512:(n % 2 + 1) * 512]
                dreg = ("vec", order[n][v], n % 2)
                P.tt("dve", dst, P.bank(v), self.adab, ALU.add, [("ps", v), "adab"], [dreg])
                if n in (2, 3):
                    P.stt("dve", dst, dst, 1.0, self.gbc[:, (n % 2) * 512:(n % 2 + 1) * 512], ALU.add, ALU.mult,
                          [dreg, "gbc"], [dreg])

    def vreg(self, nm):
        return [("vec", nm, 0), ("vec", nm, 1)]

    def norm_all(self, srcs, kinds, hlT):
        P = self.P
        n = len(srcs)
        ssa = self.ss_all
        junk = self.junk
        for i, src in enumerate(srcs):
            b = i % 2
            xs = self.xs[b]
            P.ld(xs, src, w=[("xs", b)])
            P.actf(junk, xs, AF.Square, [("xs", b)], ["junk", "ss_all"], accum=ssa[:, i:i + 1])
        P.actf(junk[:, 0:8], self.xs[(n - 1) % 2][:, 0:8], AF.Square, [("xs", (n - 1) % 2)], ["junk", "ss_all"],
               accum=ssa[:, 31:32])
        P.ts("dve", ssa[:, 0:n], ssa[:, 0:n], 1.0 / D, EPS, ALU.mult, ALU.add, ["ss_all"], ["ss_all"])
        P.actf(ssa[:, 0:n], ssa[:, 0:n], AF.Ln, ["ss_all"], ["ss_all"])
        P.actf(ssa[:, 0:n], ssa[:, 0:n], AF.Exp, ["ss_all"], ["ss_all"], scale=-0.5)
        for i, src in enumerate(srcs):
            b = i % 2
            gs, sh = kinds[i]
            xs = self.xs[b]
            P.ld(xs, src, w=[("xs", b)])
            tmp = self.ntmp
            P.stt("dve", tmp, xs, ssa[:, i:i + 1], self.vec[gs], ALU.mult, ALU.mult,
                  [("xs", b), "ss_all"] + self.vreg(gs), ["ntmp"])
            hlb = self.hlb[b]
            P.tt("pool", hlb, tmp, self.vec[sh], ALU.add, ["ntmp"] + self.vreg(sh), [("hlb", b)])
            pst = P.bank(2 + b).bitcast(BF16)
            P.tr([(pst[:, k * 128:(k + 1) * 128], hlb[:, k * 128:(k + 1) * 128], self.ident_b) for k in range(8)],
                 [("hlb", b), "ident_b"], [("ps", 2 + b)])
            P.cp("dve", hlT[:, :, i * 128:(i + 1) * 128], pst.rearrange("p (k t) -> p k t", k=8),
                 [("ps", 2 + b)], [("hlT", i)])

    def norm_bufs(self):
        P = self.P
        self.xs = [P.sb("xs%d" % i, [128, 1024], F32) for i in range(2)]
        self.junk = P.sb("junk", [128, 1024], BF16)
        self.ss_all = P.sb("ss_all", [128, 32], F32)
        self.ntmp = P.sb("ntmp", [128, 1024], F32)
        self.hlb = [P.sb("hlb%d" % i, [128, 1024], BF16) for i in range(2)]

    def l0_phase1(self):
        P = self.P
        st = self.stage
        full = st != "A"
        xo_d = self.din("xo", [NT, D])
        xh_d = self.din("xh", [256, D])
        ctx_d = self.din("ctx", [CTX, D])
        self.xo_d, self.ctx_d = xo_d, ctx_d
        w_in_d = self.din("w_in0", [D, 3328])
        cos_d = self.din("rope_cos", [128, 2304])
        sin_d = self.din("rope_sin", [128, 2304])
        wv = w_in_d.rearrange("(k p) n -> p k n", p=128)

        import os
        dbg = int(os.environ.get("DBG_STOP", "99"))
        P.sb_off = self.ZA
        if st == "A":
            self.kt_sh = self.dout("kt_sh", [512, NT], BF16)
            self.v_sh = self.dout("v_sh", [512, NT], BF16)
        if dbg <= 0:
            return
        self.adaln(0)
        P.barrier()
        if dbg <= 1:
            return
        P.sb_off = self.ZA
        NCOL = 20 * 128
        hlT = P.sb("hlT", [128, 8, NCOL], BF16)
        wst = [P.sb("wst%d" % i, [128, 8, 128], F32) for i in range(2)]
        wbf = [P.sb("wbf%d" % i, [128, 8, 256], BF16) for i in range(2)]
        t1 = P.sb("rp_t1", [128, 512], F32)
        t2 = P.sb("rp_t2", [128, 512], F32)
        sqb = P.sb("rp_sq", [128, 512], BF16)
        kst = [P.sb("kst%d" % i, [128, 512], BF16) for i in range(2)]
        smax = P.sb("rp_smax", [128, 1], F32)
        assert P.sb_off <= self.ZP, P.sb_off

        P.sb_off = self.ZP
        self.QAT = P.sb("QAT", [128, 4, 2304], BF16)
        self.KAT = P.sb("KAT", [128, 2, 2560], BF16)
        self.VA = P.sb("VA", [128, 20, 128], BF16)
        self.QBT = P.sb("QBT", [128, 4, 2304], BF16)
        self.KBTc = P.sb("KBTc", [128, 4, 256], BF16)
        self.VBc = P.sb("VBc", [128, 2, 512], BF16)
        self.stat = P.sb("stat", [128, 4], F32)
        assert P.sb_off <= self.ZB, P.sb_off
        P.op("pool", lambda e: e.memset(self.stat, 0.0), writes=["stat"])

        P.sb_off = self.ZB
        self.norm_bufs()
        cosT = P.sb("cosT", [128, 2304], F32)
        sinT = P.sb("sinT", [128, 2304], F32)
        wva_st = P.sb("wva_st", [128, 8, 128], F32)
        wva = P.sb("wva", [128, 8, 128], BF16)
        wvb = P.sb("wvb", [128, 8, 512], BF16)
        wvb_st = P.sb("wvb_st", [128, 2, 512], F32)
        vst = [P.sb("vst%d" % i, [128, 512], BF16) for i in range(2)]

        srcs = [xh_d[0:128, :]] + [xo_d[t * 128:(t + 1) * 128, :] for t in range(NTILE)] + [xh_d[128:256, :]] + \
               [ctx_d[0:128, :], ctx_d[128:256, :]]
        kinds = [("gs", "sh")] * 18 + [("gsc", "shc")] * 2
        self.norm_all(srcs, kinds, hlT)
        P.ld(cosT, cos_d, w=["cosT"])
        P.ld(sinT, sin_d, w=["sinT"])

        if dbg <= 2:
            return
        perm = [(16, 0), (0, 16), (48, 32), (32, 48)]
        self.chunk_i = 0

        def load_chunk(colspec, roped):
            i = self.chunk_i
            self.chunk_i += 1
            b = i % 2
            off = 0
            for (c0, wdt) in colspec:
                P.ld(wst[b][:, :, off:off + wdt], wv[:, :, c0:c0 + wdt], w=[("wst", b)])
                off += wdt
            P.cp("pool", wbf[b][:, :, 0:128], wst[b], [("wst", b)], [("wbf", b)])
            if roped:
                for blk in range(2):
                    for (so, do) in perm:
                        P.cp("pool", wbf[b][:, :, 128 + blk * 64 + do:128 + blk * 64 + do + 16],
                             wst[b][:, :, blk * 64 + so:blk * 64 + so + 16], [("wst", b)], [("wbf", b)])
            return b

        def fm_group(b, c0, n, rc, gi, dst, dreg, statcol=None, silu=False):
            pb = 4 + 2 * (gi % 2)
            hregs = [("hlT", t) for t in range(c0 // 128, (c0 + n) // 128)]
            items = [(P.bank(pb)[:, 0:n], wbf[b][:, k, 0:128], hlT[:, k, c0:c0 + n], k == 0, k == 7) for k in range(8)]
            P.mm(items, [("wbf", b)] + hregs, [("ps", pb)])
            if rc is not None:
                items = [(P.bank(pb + 1)[:, 0:n], wbf[b][:, k, 128:256], hlT[:, k, c0:c0 + n], k == 0, k == 7) for k in range(8)]
                P.mm(items, [("wbf", b)] + hregs, [("ps", pb + 1)])
                P.tt("dve", t1[:, 0:n], P.bank(pb)[:, 0:n], cosT[:, rc:rc + n], ALU.mult, [("ps", pb), "cosT"], ["rp_t1"])
                P.tt("dve", t2[:, 0:n], P.bank(pb + 1)[:, 0:n], sinT[:, rc:rc + n], ALU.mult, [("ps", pb + 1), "sinT"], ["rp_t2"])
                P.tt("pool", dst, t1[:, 0:n], t2[:, 0:n], ALU.add, ["rp_t1", "rp_t2"], [dreg])
            elif silu:
                P.actf(t1[:, 0:n], P.bank(pb)[:, 0:n], AF.Exp, [("ps", pb)], ["rp_t1"], scale=-1.0)
                P.ts("pool", t1[:, 0:n], t1[:, 0:n], 1.0, None, ALU.add, None, ["rp_t1"], ["rp_t1"])
                P.op("dve", lambda e: e.reciprocal(out=t2[:, 0:n], in_=t1[:, 0:n]), ["rp_t1"], ["rp_t2"])
                P.tt("dve", dst, P.bank(pb)[:, 0:n], t2[:, 0:n], ALU.mult, [("ps", pb), "rp_t2"], [dreg])
            else:
                P.cp("act", dst, P.bank(pb)[:, 0:n], [("ps", pb)], [dreg])
            if statcol is not None:
                P.tt("pool", sqb[:, 0:n], dst, dst, ALU.mult, [dreg], ["rp_sq"])
                P.mm([(P.bank(pb + 1)[:, 0:n], self.blk_b, sqb[:, 0:n], True, True)],
                     ["rp_sq", "blk_b"], [("ps", pb + 1)])
                P.op("dve", lambda e: e.tensor_reduce(out=smax, in_=P.bank(pb + 1)[:, 0:n], axis=AX.X, op=ALU.max),
                     [("ps", pb + 1)], ["rp_smax"])
                P.tt("dve", self.stat[:, statcol:statcol + 1], self.stat[:, statcol:statcol + 1], smax, ALU.max,
                     ["rp_smax", "stat"], ["stat"])

        own_groups = [(128 + 512 * g, 512, 128 + 512 * g) for g in range(4)] + [(2304, 256, None)]
        ka_groups = [(512 * g, 512, 512 * g) for g in range(4)] + [(2048, 256, 2048), (2304, 256, None)]

        def own_dst(T, c, nm, gi):
            if gi < 4:
                return T[:, c, 512 * gi:512 * gi + 512], (nm, c, gi)
            return T[:, c, 2048:2304], (nm, c, 4)

        P.ld(wva_st, wv[:, :, 640:768], w=["wva_st"])
        P.cp("pool", wva, wva_st, ["wva_st"], ["wva"])
        for k2 in range(4):
            P.ld(wvb_st, wv[:, 2 * k2:2 * k2 + 2, 1792:2304], w=["wvb_st"])
            P.cp("pool", wvb[:, 2 * k2:2 * k2 + 2, :], wvb_st, ["wvb_st"], [("wvb", k2)])
        for i in range(20):
            c0 = i * 128
            if full:
                items = [(P.bank(4)[:, 0:128], hlT[:, k, c0:c0 + 128], wva[:, k, :], k == 0, k == 7) for k in range(8)]
                P.mm(items, [("hlT", i), "wva"], [("ps", 4)])
                P.cp("act", self.VA[:, i, :], P.bank(4)[:, 0:128], [("ps", 4)], [("VA", i)])
            own = 1 <= i <= 16
            if (own and st == "A") or (i >= 18 and full):
                items = [(P.bank(5), hlT[:, k, c0:c0 + 128], wvb[:, k, :], k == 0, k == 7) for k in range(8)]
                P.mm(items, [("hlT", i)] + [("wvb", k2) for k2 in range(4)], [("ps", 5)])
                if i >= 18:
                    P.cp("act", self.VBc[:, i - 18, :], P.bank(5), [("ps", 5)], [("VBc", i - 18)])
                else:
                    j = i % 2
                    t = i - 1
                    P.cp("act", vst[j], P.bank(5), [("ps", 5)], [("vst", j)])
                    dstv = self.v_sh.rearrange("(h p) (t d) -> p h t d", p=128, d=128)[:, :, t, :]
                    P.dma("sp", lambda e, j=j, dstv=dstv: e.dma_start(out=dstv, in_=vst[j].rearrange("p (h d) -> p h d", h=4)),
                          reads=[("vst", j)])

        if dbg <= 3:
            return
        if full:
            for c in range(4):
                b = load_chunk([(128 * c, 128)], True)
                for gi, (c0, n, rc) in enumerate(own_groups):
                    dst, dreg = own_dst(self.QAT, c, "QAT", gi)
                    fm_group(b, c0, n, rc, gi, dst, dreg, statcol=0)
            for kv in range(2):
                b = load_chunk([(512 + 64 * kv, 64), (512 + 64 * kv, 64)], True)
                for gi, (c0, n, rc) in enumerate(ka_groups):
                    fm_group(b, c0, n, rc, gi, self.KAT[:, kv, c0:c0 + n], ("KAT", kv, gi), statcol=1)
            for c in range(4):
                b = load_chunk([(768 + 128 * c, 128)], True)
                for gi, (c0, n, rc) in enumerate(own_groups):
                    dst, dreg = own_dst(self.QBT, c, "QBT", gi)
                    fm_group(b, c0, n, rc, gi, dst, dreg, statcol=2)
        kst_i = 0
        for c in range(4):
            b = load_chunk([(1280 + 128 * c, 128)], True)
            for gi, (c0, n, rc) in enumerate(own_groups):
                if gi < 4:
                    j = kst_i % 2
                    kst_i += 1
                    fm_group(b, c0, n, rc, gi, kst[j], ("kst", j), statcol=3)
                    if st == "A":
                        P.dma("sp", lambda e, j=j, c=c, gi=gi: e.dma_start(
                            out=self.kt_sh[c * 128:(c + 1) * 128, gi * 512:(gi + 1) * 512], in_=kst[j]), reads=[("kst", j)])
                elif full:
                    fm_group(b, c0, n, rc, gi, self.KBTc[:, c, :], ("KBTc", c), statcol=3)
        if not full:
            return
        P.barrier()
        P.sb_off = self.ZB
        self.GT = P.sb("GT", [128, 8, 2304], BF16)
        self.ZB2 = P.sb_off
        for c in range(8):
            b = load_chunk([(2304 + 128 * c, 128)], False)
            for gi, (c0, n, rc) in enumerate(own_groups):
                dst, dreg = own_dst(self.GT, c, "GT", gi)
                fm_group(b, c0, n, None, gi, dst, dreg, silu=True)
        P.barrier()

    def l0_phase2(self):
        P = self.P
        kt_full = self.din("kt_full", [4096, NT], BF16)
        v_full = self.din("v_full", [4096, NT], BF16)
        w_out_d = self.din("w_out0", [D, D])
        sink_bc_d = self.din("sink_bc", [128, 4])
        sink_row_d = self.din("sink_row", [1, 8])
        lamv_d = self.din("lamv", [1, 256])
        subln_d = self.din("subln", [128, 1])
        masks_d = self.din("masks", [128, 640])
        xl1_d = self.dout("xl1", [NT, D])
        xc1_d = self.dout("xc1", [CTX, D])
        GT = self.GT

        P.sb_off = self.ZA
        KTq = [P.sb("KTq%d" % i, [128, 4096], BF16) for i in range(2)]
        Vq = [P.sb("Vq%d" % i, [128, 32, 128], BF16) for i in range(2)]
        f1 = P.sb("f1", [128, 512], F32)
        f2 = P.sb("f2", [128, 512], F32)
        sqb = P.sb("sqb2", [128, 512], BF16)
        Pb = [P.sb("Pb%d" % i, [128, 1024], BF16) for i in range(2)]
        PA = [P.sb("PA%d" % i, [128, 640], BF16) for i in range(2)]
        PB = [P.sb("PB%d" % i, [128, 640], BF16) for i in range(2)]
        rl = P.sb("rl", [128, 128], F32)
        ot = P.sb("ot", [128, 128], F32)
        mask_f = P.sb("mask_f", [128, 640], F32)
        masks = [P.sb("mask%d" % i, [128, 640], BF16) for i in range(3)]
        xs = [P.sb("xs2_%d" % i, [128, 1024], F32) for i in range(2)]
        small = P.sb("small", [128, 64], F32)
        row = P.sb("rowsc", [1, 512], F32)
        lamv = P.sb("lamv", [1, 256], F32)
        sinkr = P.sb("sinkr", [1, 8], F32)
        sink_bc = P.sb("sink_bc", [128, 4], F32)
        gsub = P.sb("gsub", [128, 1], F32)
        assert P.sb_off <= self.ZP, P.sb_off
        P.sb_off = self.ZB2
        Wout = P.sb("Wout", [128, 8, 1024], BF16)
        wo_st = P.sb("wo_st", [128, 1024], F32)
        w_out_v = w_out_d.rearrange("(k p) n -> p k n", p=128)
        for k in range(8):
            P.ld(wo_st, w_out_v[:, k, :], w=["wo_st"])
            P.cp("pool", Wout[:, k, :], wo_st, ["wo_st"], [("Wout", k)])

        negc = small[:, 0:3]
        esink = small[:, 4:8]
        P.ld(lamv, lamv_d, w=["lamv"])
        P.ld(sinkr, sink_row_d, w=["sinkr"])
        P.ld(sink_bc, sink_bc_d, w=["sink_bc"])
        P.ld(gsub, subln_d, w=["gsub"])
        P.ts("dve", gsub, gsub, 0.8, None, ALU.mult, None, ["gsub"], ["gsub"])
        P.ld(mask_f, masks_d, w=["mask_f"])
        for i in range(3):
            P.cp("dve", masks[i], mask_f, ["mask_f"], [("mask", i)])
        P.ts("dve", masks[0][:, 0:128], masks[0][:, 0:128], self.flags[:, 0:1], None, ALU.mult, None,
             [("mask", 0), "flags"], [("mask", 0)])
        P.ts("dve", masks[2][:, 256:384], masks[2][:, 256:384], self.flags[:, 1:2], None, ALU.mult, None,
             [("mask", 2), "flags"], [("mask", 2)])
        P.tr([(P.bank(0)[0:1, j * 128:(j + 1) * 128], self.stat[:, j:j + 1], self.ident_f) for j in range(4)],
             ["stat", "ident_f"], [("ps", 0)])
        P.op("dve", lambda e: e.tensor_reduce(out=row[:, 0:4], in_=P.bank(0)[0:1, :].rearrange("p (a b) -> p a b", a=4),
                                              axis=AX.X, op=ALU.max), [("ps", 0)], ["row"])
        P.tt("dve", row[:, 8:9], row[:, 0:1], row[:, 1:2], ALU.mult, ["row"], ["row"])
        P.tt("dve", row[:, 9:10], row[:, 2:3], row[:, 3:4], ALU.mult, ["row"], ["row"])
        P.actf(row[:, 8:10], row[:, 8:10], AF.Ln, ["row"], ["row"])
        P.actf(row[:, 8:10], row[:, 8:10], AF.Exp, ["row"], ["row"], scale=0.5, bias=math.log(1.5 / 8.0))
        P.op("dve", lambda e: e.tensor_reduce(out=row[:, 10:11], in_=sinkr, axis=AX.X, op=ALU.max), ["sinkr"], ["row"])
        P.tt("dve", row[:, 8:9], row[:, 8:9], row[:, 10:11], ALU.max, ["row"], ["row"])
        P.tt("dve", lamv[:, 0:64], lamv[:, 0:64], lamv[:, 64:128], ALU.mult, ["lamv"], ["lamv"])
        P.tt("dve", lamv[:, 128:192], lamv[:, 128:192], lamv[:, 192:256], ALU.mult, ["lamv"], ["lamv"])
        P.op("dve", lambda e: e.tensor_reduce(out=row[:, 12:13], in_=lamv[:, 0:64], axis=AX.X, op=ALU.add), ["lamv"], ["row"])
        P.op("dve", lambda e: e.tensor_reduce(out=row[:, 13:14], in_=lamv[:, 128:192], axis=AX.X, op=ALU.add), ["lamv"], ["row"])
        P.actf(row[:, 12:14], row[:, 12:14], AF.Exp, ["row"], ["row"])
        P.tt("dve", row[:, 14:15], row[:, 12:13], row[:, 13:14], ALU.subtract, ["row"], ["row"])
        P.ts("dve", row[:, 16:18], row[:, 8:10], -1.0, None, ALU.mult, None, ["row"], ["row"])
        P.ts("dve", row[:, 18:19], row[:, 14:15], -1.0, -0.2, ALU.mult, ALU.add, ["row"], ["row"])
        P.mm([(P.bank(1)[:, 0:3], self.ones_f[0:1, 0:128], row[0:1, 16:19], True, True)], ["row", "ones_f"], [("ps", 1)])
        P.cp("dve", negc, P.bank(1)[:, 0:3], [("ps", 1)], ["negc"])
        P.actf(esink, sink_bc, AF.Exp, ["sink_bc", "negc"], ["esink"], bias=negc[:, 0:1])

        import os
        ph2 = int(os.environ.get("PH2_STOP", "99"))
        if ph2 <= 1:
            return
        def kat_reg(kv, ti):
            return ("KAT", kv, ti // 4 if ti < 16 else (4 if ti < 18 else 5))

        wi = 0
        for qt in range(18):
            own = qt < 16
            tiles = [qt, qt + 1, qt + 2, 18, 19] if own else [18, 19]
            nk = len(tiles)
            W = nk * 128
            qc0 = qt * 128 if own else 2048 + (qt - 16) * 128
            gi = qt // 4 if own else 4
            mk = None if not own else (0 if qt == 0 else (2 if qt == 15 else 1))
            for c in range(4):
                kv = c // 2
                b = wi % 2
                wi += 1
                items = []
                for hb in range(2):
                    base = 64 * hb
                    for j, ti in enumerate(tiles):
                        items.append((P.psum[:, hb * 1024 + j * 128: hb * 1024 + (j + 1) * 128],
                                      self.KAT[base:base + 64, kv, ti * 128:(ti + 1) * 128],
                                      self.QAT[base:base + 64, c, qc0:qc0 + 128], True, True))
                kregs = sorted(set(kat_reg(kv, ti) for ti in tiles))
                P.mm(items, [("QAT", c, gi)] + kregs, [("ps", 0), ("ps", 1), ("ps", 2), ("ps", 3)])
                P.actf(PA[b][:, 0:W], P.psum[:, 0:W], AF.Exp, [("ps", 0), ("ps", 1), "negc"], [("PA", b)],
                       bias=negc[:, 0:1], scale=0.125)
                P.actf(PB[b][:, 0:W], P.psum[:, 1024:1024 + W], AF.Exp, [("ps", 2), ("ps", 3), "negc"], [("PB", b)],
                       bias=negc[:, 0:1], scale=0.125)
                if mk is not None:
                    P.tt("dve", PA[b][:, 0:W], PA[b][:, 0:W], masks[mk], ALU.mult, [("PA", b), ("mask", mk)], [("PA", b)])
                    P.tt("pool", PB[b][:, 0:W], PB[b][:, 0:W], masks[mk], ALU.mult, [("PB", b), ("mask", mk)], [("PB", b)])
                ob = 4 + b
                OW = P.bank(ob)
                items = []
                for hb, PP in ((0, PA[b]), (1, PB[b])):
                    lo = 64 * hb
                    for j, ti in enumerate(tiles):
                        items.append((OW[lo:lo + 64, 0:128], self.VA[:, ti, kv * 64:(kv + 1) * 64], PP[:, j * 128:(j + 1) * 128],
                                      j == 0, j == nk - 1))
                    for j, ti in enumerate(tiles):
                        items.append((OW[lo:lo + 64, 128:256], self.ones_b[:, 0:64], PP[:, j * 128:(j + 1) * 128],
                                      j == 0, j == nk - 1))
                P.mm(items, [("PA", b), ("PB", b), "ones_b"] + [("VA", ti) for ti in tiles], [("ps", ob)])
                P.ts("dve", rl, OW[:, 128:256], esink[:, c:c + 1], None, ALU.add, None, [("ps", ob), "esink"], ["rl"])
                P.op("dve", lambda e: e.reciprocal(out=rl, in_=rl), ["rl"], ["rl"])
                P.tt("dve", ot, OW[:, 0:128], rl, ALU.mult, [("ps", ob), "rl"], ["ot"])
                P.tt("pool", GT[:, c, qc0:qc0 + 128], ot, GT[:, c, qc0:qc0 + 128], ALU.mult, ["ot", ("GT", c, gi)], [("GT", c, gi)])

        if ph2 <= 2:
            return
        segs = [(h, g, qq) for h in range(4) for g in range(4) for qq in range(4)]

        def load_seg(si):
            h, g, qq = segs[si]
            b = si % 2
            for rr in range(2):
                r = 2 * qq + rr
                P.ld(KTq[b][:, rr * 2048:(rr + 1) * 2048], kt_full[r * 512 + h * 128: r * 512 + (h + 1) * 128, :], w=[("KTq", b)])
                P.ld(Vq[b][:, rr * 16:(rr + 1) * 16, :],
                     v_full[r * 512 + h * 128: r * 512 + (h + 1) * 128, :].rearrange("p (t d) -> p t d", d=128), w=[("Vq", b)])

        load_seg(0)
        load_seg(1)
        self.kidx = 0

        def attend(h, qcols, N, qreg, klist):
            n = len(klist)

            def qk(k):
                sb_ = (self.kidx + k) % 2
                kt_ap, v_ap, regs, _ = klist[k]
                base = sb_ * 1024
                P.mm([(P.psum[:, base:base + N], kt_ap[0:64, :], self.QBT[0:64, h, qcols], True, True),
                      (P.psum[:, base + 512:base + 512 + N], kt_ap[64:128, :], self.QBT[64:128, h, qcols], True, True)],
                     [qreg] + regs, [("ps", 2 * sb_), ("ps", 2 * sb_ + 1)])

            qk(0)
            for k in range(n):
                sb_ = (self.kidx + k) % 2
                if k + 1 < n:
                    qk(k + 1)
                kt_ap, v_ap, regs, after = klist[k]
                base = sb_ * 1024
                src = P.psum[:, base:base + 1024].rearrange("p (a b) -> p a b", a=2)[:, :, 0:N]
                dst = Pb[sb_].rearrange("p (a b) -> p a b", a=2)[:, :, 0:N]
                P.actf(dst, src, AF.Exp, [("ps", 2 * sb_), ("ps", 2 * sb_ + 1), "negc"], [("Pb", sb_)],
                       bias=negc[:, 1:2], scale=0.125)
                st_, sp_ = (k == 0), (k == n - 1)
                P.mm([(P.bank(4)[:, 0:N], v_ap, Pb[sb_][:, 0:N], st_, sp_),
                      (P.bank(6)[:, 0:N], self.ones_b, Pb[sb_][:, 0:N], st_, sp_),
                      (P.bank(5)[:, 0:N], v_ap, Pb[sb_][:, 512:512 + N], st_, sp_),
                      (P.bank(7)[:, 0:N], self.ones_b, Pb[sb_][:, 512:512 + N], st_, sp_)],
                     [("Pb", sb_), "ones_b"] + regs, [("ps", 4), ("ps", 5), ("ps", 6), ("ps", 7)])
                if after is not None:
                    after()
            self.kidx += n

        def finalize(h, zc0, N, gi):
            O1, O2, L1, L2 = P.bank(4)[:, 0:N], P.bank(5)[:, 0:N], P.bank(6)[:, 0:N], P.bank(7)[:, 0:N]
            a, b_ = f1[:, 0:N], f2[:, 0:N]
            P.op("dve", lambda e: e.reciprocal(out=a, in_=L1), [("ps", 6)], ["f1"])
            P.tt("dve", a, O1, a, ALU.mult, [("ps", 4), "f1"], ["f1"])
            P.op("dve", lambda e: e.reciprocal(out=b_, in_=L2), [("ps", 7)], ["f2"])
            P.tt("dve", b_, O2, b_, ALU.mult, [("ps", 5), "f2"], ["f2"])
            P.stt("dve", a, b_, negc[:, 2:3], a, ALU.mult, ALU.add, ["f1", "f2", "negc"], ["f1"])
            P.tt("pool", sqb[:, 0:N], a, a, ALU.mult, ["f1"], ["sqb2"])
            P.mm([(P.bank(6)[:, 0:N], self.ones_b, sqb[:, 0:N], True, True)], ["sqb2", "ones_b"], [("ps", 6)])
            P.ts("dve", b_, P.bank(6)[:, 0:N], 1.0 / 128.0, EPS, ALU.mult, ALU.add, [("ps", 6)], ["f2"])
            P.actf(b_, b_, AF.Ln, ["f2"], ["f2"])
            P.actf(b_, b_, AF.Exp, ["f2"], ["f2"], scale=-0.5)
            P.tt("dve", a, a, b_, ALU.mult, ["f1", "f2"], ["f1"])
            P.stt("dve", GT[:, 4 + h, zc0:zc0 + N], a, gsub[:, 0:1], GT[:, 4 + h, zc0:zc0 + N], ALU.mult, ALU.mult,
                  ["f1", "gsub", ("GT", 4 + h, gi)], [("GT", 4 + h, gi)])

        si = 0
        for h in range(4):
            for g in range(4):
                klist = []
                for qq in range(4):
                    b = si % 2
                    for t in range(32):
                        after = None
                        if t == 31:
                            def after(si=si):
                                if si + 2 < len(segs):
                                    load_seg(si + 2)
                        klist.append((KTq[b][:, t * 128:(t + 1) * 128], Vq[b][:, t, :], [("KTq", b), ("Vq", b)], after))
                    si += 1
                for j in range(2):
                    klist.append((self.KBTc[:, h, j * 128:(j + 1) * 128], self.VBc[:, j, h * 128:(h + 1) * 128],
                                  [("KBTc", h), ("VBc", j)], None))
                attend(h, slice(512 * g, 512 * g + 512), 512, ("QBT", h, g), klist)
                finalize(h, 512 * g, 512, g)
            klist = [(self.KBTc[:, h, j * 128:(j + 1) * 128], self.VBc[:, j, h * 128:(h + 1) * 128],
                      [("KBTc", h), ("VBc", j)], None) for j in range(2)]
            attend(h, slice(2048, 2304), 256, ("QBT", h, 4), klist)
            finalize(h, 2048, 256, 4)

        if ph2 <= 3:
            return
        for t in range(18):
            own = t < 16
            c0 = t * 128 if own else 2048 + (t - 16) * 128
            gi = t // 4 if own else 4
            b = t % 2
            src = self.xo_d[t * 128:(t + 1) * 128, :] if own else self.ctx_d[(t - 16) * 128:(t - 15) * 128, :]
            dstd = xl1_d[t * 128:(t + 1) * 128, :] if own else xc1_d[(t - 16) * 128:(t - 15) * 128, :]
            gt = self.vec["gt" if own else "gtc"]
            gtn = "gt" if own else "gtc"
            P.ld(xs[b], src, w=[("xs2", b)])
            for half in range(2):
                pb = 2 * b + half
                P.mm([(P.bank(pb), GT[:, c, c0:c0 + 128], Wout[:, c, half * 512:(half + 1) * 512], c == 0, c == 7) for c in range(8)],
                     [("GT", c, gi) for c in range(8)] + [("Wout", k) for k in range(8)], [("ps", pb)])
                P.tt("dve", f1, P.bank(pb), gt[:, half * 512:(half + 1) * 512], ALU.mult, [("ps", pb), ("vec", gtn, half)], ["f1"])
                P.tt("pool", xs[b][:, half * 512:(half + 1) * 512], xs[b][:, half * 512:(half + 1) * 512], f1, ALU.add,
                     ["f1", ("xs2", b)], [("xs2", b)])
            P.dma("sp", lambda e, dstd=dstd, b=b: e.dma_start(out=dstd, in_=xs[b]), reads=[("xs2", b)])

    def l1_local(self):
        P = self.P
        xl1_d = self.din("xl1_in", [NT, D])
        xl1h_d = self.din("xl1_halo", [128, D])
        xc1_d = self.din("xc1_in", [CTX, D])
        w_in_d = self.din("w_in1", [D, 2 * RW])
        convw_d = self.din("convw_t", [BD, NB, 4])
        chan_d = self.din("chan_t", [BD, NB, 8])
        wbd_d = self.din("wbd_t", [BD, NB, 4, BD])
        spill_d = self.dout("spill", [NB, 3, BD, NT])
        summ_d = self.dout("summ", [BD, NB * 6])
        wv = w_in_d.rearrange("(k p) n -> p k n", p=128)

        P.sb_off = self.ZA
        self.adaln(1)
        P.barrier()
        P.sb_off = self.ZA
        NCOL = 19 * 128
        hlT = P.sb("hl1T", [128, 8, NCOL], BF16)
        self.norm_bufs()
        srcs = [xl1_d[t * 128:(t + 1) * 128, :] for t in range(NTILE)] + [xl1h_d] + [xc1_d[0:128, :], xc1_d[128:256, :]]
        kinds = [("gs", "sh")] * 17 + [("gsc", "shc")] * 2
        self.norm_all(srcs, kinds, hlT)
        P.barrier()
        P.sb_off = self.ZA + 8 * NCOL * 2
        P.sb_off = (P.sb_off + 63) // 64 * 64

        convw = P.sb("convw", [BD, NB, 4], F32)
        chan = P.sb("chan", [BD, NB, 8], F32)
        cs = P.sb("cs", [BD, NB, 2], F32)
        summ = P.sb("summ", [BD, NB, 6], F32)
        zeros = P.sb("zeros", [BD, NT], F32)
        P.ld(convw, convw_d, w=["convw"])
        P.ld(chan, chan_d, w=["chan"])
        P.op("pool", lambda e: e.memset(zeros, 0.0), writes=["zeros"])
        P.op("pool", lambda e: e.memset(summ, 0.0), writes=["summ"])
        for d in range(2):
            lamc = chan[:, :, 3 + 3 * d]
            P.actf(cs[:, :, d], lamc, AF.Exp, ["chan"], ["cs"], scale=-1.0)
            P.actf(cs[:, :, d], cs[:, :, d], AF.Ln, ["cs"], ["cs"], bias=1.0)
            P.ts("dve", cs[:, :, d], cs[:, :, d], -8.0, None, ALU.mult, None, ["cs"], ["cs"])

        wst = [P.sb("w1st%d" % i, [128, 8, 160], F32) for i in range(2)]
        wbf = [P.sb("w1bf%d" % i, [128, 8, 160], BF16) for i in range(2)]
        wbd_st = [P.sb("wbdst%d" % i, [BD, 4, BD], F32) for i in range(2)]
        wbd = [P.sb("wbd%d" % i, [BD, 4, BD], BF16) for i in range(2)]
        xe = P.sb("xe", [BD, NT + 3], F32)
        xce = P.sb("xce", [BD, CTX + 3], F32)
        u = P.sb("u", [BD, NT], F32)
        uc = P.sb("uc", [BD, CTX], F32)
        ub = P.sb("ub", [BD, NT], BF16)
        ucb = P.sb("ucb", [BD, CTX], BF16)
        sg = P.sb("sg", [BD, NT], F32)
        rr = P.sb("rr", [BD, NT], F32)
        ii = P.sb("ii", [BD, NT], F32)
        aa = P.sb("aa", [BD, NT], F32)
        bb = P.sb("bb", [BD, NT], F32)
        Sacc = P.sb("Sacc", [BD, NT], F32)
        Pf = P.sb("Pf", [BD, NT], F32)
        Pr = P.sb("Pr", [BD, NT], F32)
        rc = P.sb("rc", [BD, CTX], F32)
        ic = P.sb("ic", [BD, CTX], F32)
        ac = P.sb("ac", [BD, CTX], F32)
        bc = P.sb("bc", [BD, CTX], F32)
        hc = P.sb("hc", [BD, CTX], F32)
        P.op("pool", lambda e: e.memset(xce, 0.0), writes=["xce"])

        def rev(ap, d):
            return ap[:, ::-1] if d == 1 else ap

        for h in range(NB):
            b = h % 2
            P.ld(wst[b][:, :, 0:80], wv[:, :, BD * h:BD * h + BD], w=[("w1st", b)])
            P.ld(wst[b][:, :, 80:160], wv[:, :, RW + BD * h:RW + BD * h + BD], w=[("w1st", b)])
            P.cp("pool", wbf[b], wst[b], [("w1st", b)], [("w1bf", b)])
            P.ld(wbd_st[b], wbd_d[:, h, :, :], w=[("wbdst", b)])
            P.cp("pool", wbd[b], wbd_st[b], [("wbdst", b)], [("wbd", b)])
            hregs = [("hlT", t) for t in range(19)]
            for g in range(4):
                pb = g % 2
                P.mm([(P.bank(pb)[0:BD, :], wbf[b][:, k, 0:80], hlT[:, k, 512 * g:512 * g + 512], k == 0, k == 7) for k in range(8)],
                     [("w1bf", b)] + hregs, [("ps", pb)])
                P.cp("act", xe[:, 2 + 512 * g:2 + 512 * g + 512], P.bank(pb)[0:BD, :], [("ps", pb)], ["xe"])
            P.mm([(P.bank(2)[0:BD, 0:4], wbf[b][:, k, 0:80], hlT[:, k, 2048:2052], k == 0, k == 7) for k in range(8)],
                 [("w1bf", b)] + hregs, [("ps", 2)])
            P.ts("dve", xe[:, 0:2], P.bank(2)[0:BD, 0:2], self.flags[0:BD, 0:1], None, ALU.mult, None, [("ps", 2), "flags"], ["xe"])
            P.ts("dve", xe[:, NT + 2:NT + 3], P.bank(2)[0:BD, 2:3], self.flags[0:BD, 1:2], None, ALU.mult, None,
                 [("ps", 2), "flags"], ["xe"])
            P.mm([(P.bank(3)[0:BD, 0:CTX], wbf[b][:, k, 0:80], hlT[:, k, 2176:2432], k == 0, k == 7) for k in range(8)],
                 [("w1bf", b)] + hregs, [("ps", 3)])
            P.cp("act", xce[:, 2:2 + CTX], P.bank(3)[0:BD, 0:CTX], [("ps", 3)], ["xce"])
            for g in range(4):
                pb = g % 2
                P.mm([(P.bank(pb)[0:BD, :], wbf[b][:, k, 80:160], hlT[:, k, 512 * g:512 * g + 512], k == 0, k == 7) for k in range(8)],
                     [("w1bf", b)] + hregs, [("ps", pb)])
                P.actf(sg[:, 512 * g:512 * g + 512], P.bank(pb)[0:BD, :], AF.Silu, [("ps", pb)], ["sg"])
            for (src, dst, n, sreg, dreg) in ((xe, u, NT, "xe", "u"), (xce, uc, CTX, "xce", "uc")):
                P.ts("dve", dst, src[:, 0:n], convw[:, h, 0:1], chan[:, h, 0:1], ALU.mult, ALU.add, [sreg, "convw", "chan"], [dreg])
                for j in range(1, 4):
                    P.stt("dve", dst, src[:, j:j + n], convw[:, h, j:j + 1], dst, ALU.mult, ALU.add, [sreg, "convw", dreg], [dreg])
            P.cp("pool", ub, u, ["u"], ["ub"])
            P.cp("pool", ucb, uc, ["uc"], ["ucb"])
            for d in range(2):
                ba = chan[:, h, 1 + 3 * d:2 + 3 * d]
                bx = chan[:, h, 2 + 3 * d:3 + 3 * d]
                csd = cs[:, h, d:d + 1]
                for g in range(4):
                    pb = 2 * (g % 2)
                    sl = slice(512 * g, 512 * g + 512)
                    P.mm([(P.bank(pb)[0:BD, :], wbd[b][:, 2 * d, :], ub[:, sl], True, True)], [("wbd", b), "ub"], [("ps", pb)])
                    P.mm([(P.bank(pb + 1)[0:BD, :], wbd[b][:, 2 * d + 1, :], ub[:, sl], True, True)], [("wbd", b), "ub"], [("ps", pb + 1)])
                    P.actf(rr[:, sl], P.bank(pb)[0:BD, :], AF.Sigmoid, [("ps", pb), "chan"], ["rr"], bias=ba)
                    P.actf(ii[:, sl], P.bank(pb + 1)[0:BD, :], AF.Sigmoid, [("ps", pb + 1), "chan"], ["ii"], bias=bx)
                P.mm([(P.bank(4)[0:BD, 0:CTX], wbd[b][:, 2 * d, :], ucb, True, True)], [("wbd", b), "ucb"], [("ps", 4)])
                P.mm([(P.bank(5)[0:BD, 0:CTX], wbd[b][:, 2 * d + 1, :], ucb, True, True)], [("wbd", b), "ucb"], [("ps", 5)])
                P.actf(rc, P.bank(4)[0:BD, 0:CTX], AF.Sigmoid, [("ps", 4), "chan"], ["rc"], bias=ba)
                P.actf(ic, P.bank(5)[0:BD, 0:CTX], AF.Sigmoid, [("ps", 5), "chan"], ["ic"], bias=bx)
                for (r_, i_, a_, b_, u_, nm) in ((rr, ii, aa, bb, u, ""), (rc, ic, ac, bc, uc, "c")):
                    P.actf(a_, r_, AF.Exp, ["rr" if nm == "" else "rc", "cs"], ["aa" + nm], scale=csd)
                    P.tt("pool", b_, a_, a_, ALU.mult, ["aa" + nm], ["bb" + nm])
                    P.ts("pool", b_, b_, -1.0, 1.0, ALU.mult, ALU.add, ["bb" + nm], ["bb" + nm])
                    P.actf(b_, b_, AF.Sqrt, ["bb" + nm], ["bb" + nm])
                    P.tt("dve", b_, b_, i_, ALU.mult, ["bb" + nm, "ii" if nm == "" else "ic"], ["bb" + nm])
                    P.tt("dve", b_, b_, u_, ALU.mult, ["bb" + nm, "u" if nm == "" else "uc"], ["bb" + nm])
                P.op("dve", lambda e, d=d: e.tensor_tensor_scan(out=rev(hc, d), data0=rev(ac, d), data1=rev(bc, d), initial=0.0,
                                                                 op0=ALU.mult, op1=ALU.add), ["aac", "bbc"], ["hc"])
                last_c = CTX - 1 if d == 0 else 0
                P.cp("pool", summ[:, h, 3 * d + 2:3 * d + 3], hc[:, last_c:last_c + 1], ["hc"], ["summ"])
                Pd = Pf if d == 0 else Pr
                pn = "Pf" if d == 0 else "Pr"
                P.op("dve", lambda e, d=d: e.tensor_tensor_scan(out=rev(rr, d), data0=rev(aa, d), data1=rev(bb, d), initial=0.0,
                                                                 op0=ALU.mult, op1=ALU.add), ["aa", "bb"], ["rr"])
                P.op("dve", lambda e, d=d, Pd=Pd: e.tensor_tensor_scan(out=rev(Pd, d), data0=rev(aa, d), data1=rev(zeros, d), initial=1.0,
                                                                        op0=ALU.mult, op1=ALU.add), ["aa", "zeros"], [pn])
                last = NT - 1 if d == 0 else 0
                P.cp("pool", summ[:, h, 3 * d:3 * d + 1], Pd[:, last:last + 1], [pn], ["summ"])
                P.cp("pool", summ[:, h, 3 * d + 1:3 * d + 2], rr[:, last:last + 1], ["rr"], ["summ"])
                if d == 0:
                    P.cp("pool", Sacc, rr, ["rr"], ["Sacc"])
                else:
                    P.tt("pool", Sacc, Sacc, rr, ALU.add, ["rr", "Sacc"], ["Sacc"])
            P.tt("dve", Sacc, Sacc, sg, ALU.mult, ["Sacc", "sg"], ["Sacc"])
            P.tt("pool", Pf, Pf, sg, ALU.mult, ["Pf", "sg"], ["Pf"])
            P.tt("dve", Pr, Pr, sg, ALU.mult, ["Pr", "sg"], ["Pr"])
            for j, (T, nm) in enumerate(((Sacc, "Sacc"), (Pf, "Pf"), (Pr, "Pr"))):
                P.dma("sp", lambda e, T=T, h=h, j=j: e.dma_start(out=spill_d[h, j], in_=T), reads=[nm])
        P.dma("sp", lambda e: e.dma_start(out=summ_d, in_=summ.rearrange("p a b -> p (a b)")), reads=["summ"])

    def l1_final(self):
        P = self.P
        xl1_d = self.din("xl1_in", [NT, D])
        spill_d = self.din("spill_in", [NB, 3, BD, NT])
        summ_d = self.din("summ_all", [NCORES, BD, NB * 6])
        sel_d = self.din("selmask", [128, 16])
        w_out_d = self.din("w_out1", [RW, D])
        fg_d = self.din("final_g", [1, D])
        out_d = self.dout("out", [NT, D])

        P.sb_off = self.ZA
        self.adaln(1)
        P.barrier()
        P.sb_off = self.ZA
        sa = P.sb("summ_all", [BD, NCORES, NB * 6], F32)
        sel = P.sb("sel", [128, 16], F32)
        hs = P.sb("hs", [BD, 2, NB], F32)
        tt_ = P.sb("foldt", [BD, NB], F32)
        P.ld(sa, summ_d.rearrange("r p n -> p r n"), w=["sa"])
        P.ld(sel, sel_d, w=["sel"])
        sav = sa.rearrange("p r (h s) -> p r h s", s=6)
        for d in range(2):
            hd = hs[:, d, :]
            P.cp("dve", hd, sav[:, 0, :, 3 * d + 2], ["sa"], ["hs"])
            order = range(NCORES) if d == 0 else range(NCORES - 1, -1, -1)
            for j in order:
                A = sav[:, j, :, 3 * d]
                B = sav[:, j, :, 3 * d + 1]
                P.tt("dve", tt_, A, hd, ALU.mult, ["sa", "hs"], ["foldt"])
                P.tt("dve", tt_, tt_, B, ALU.add, ["foldt", "sa"], ["foldt"])
                P.tt("dve", tt_, tt_, hd, ALU.subtract, ["foldt", "hs"], ["foldt"])
                P.stt("dve", hd, tt_, sel[0:BD, 8 * d + j:8 * d + j + 1], hd, ALU.mult, ALU.add, ["foldt", "sel", "hs"], ["hs"])

        W2 = P.sb("W2", [BD, NB, D], BF16)
        w2st = P.sb("w2st", [BD, 2, D], F32)
        w2v = w_out_d.rearrange("(h p) n -> p h n", p=BD)
        for h2 in range(NB // 2):
            P.ld(w2st, w2v[:, 2 * h2:2 * h2 + 2, :], w=["w2st"])
            P.cp("pool", W2[:, 2 * h2:2 * h2 + 2, :], w2st, ["w2st"], [("W2", h2)])
        Z = P.sb("Z", [BD, NB, NT], BF16)
        sp = [[P.sb("sp%d_%d" % (i, j), [BD, NT], F32) for j in range(3)] for i in range(2)]
        for h in range(NB):
            b = h % 2
            for j in range(3):
                P.ld(sp[b][j], spill_d[h, j], w=[("sp", b, j)])
            P.stt("dve", sp[b][0], sp[b][1], hs[:, 0, h:h + 1], sp[b][0], ALU.mult, ALU.add,
                  [("sp", b, 0), ("sp", b, 1), "hs"], [("sp", b, 0)])
            P.stt("dve", Z[:, h, :], sp[b][2], hs[:, 1, h:h + 1], sp[b][0], ALU.mult, ALU.add,
                  [("sp", b, 0), ("sp", b, 2), "hs"], [("Z", h)])
        fg = P.sb("fg", [128, D], F32)
        P.ld(fg, fg_d.to_broadcast([128, D]), w=["fg"])
        xs = [P.sb("xs3_%d" % i, [128, D], F32) for i in range(2)]
        junk = P.sb("junk3", [128, D], BF16)
        f1 = P.sb("f1d", [128, 512], F32)
        ssd = P.sb("ssd", [128, 32], F32)
        for t in range(NTILE):
            b = t % 2
            P.ld(xs[b], xl1_d[t * 128:(t + 1) * 128, :], w=[("xs3", b)])
            for half in range(2):
                pb = 2 * b + half
                P.mm([(P.bank(pb), Z[:, h, t * 128:(t + 1) * 128], W2[:, h, half * 512:(half + 1) * 512], h == 0, h == NB - 1)
                      for h in range(NB)], [("Z", h) for h in range(NB)] + [("W2", h2) for h2 in range(NB // 2)], [("ps", pb)])
                P.tt("dve", f1, P.bank(pb), self.vec["gt"][:, half * 512:(half + 1) * 512], ALU.mult,
                     [("ps", pb), ("vec", "gt", half)], ["f1d"])
                P.tt("pool", xs[b][:, half * 512:(half + 1) * 512], xs[b][:, half * 512:(half + 1) * 512], f1, ALU.add,
                     ["f1d", ("xs3", b)], [("xs3", b)])
            P.actf(junk, xs[b], AF.Square, [("xs3", b)], ["junk3", "ssd"], accum=ssd[:, t:t + 1])
            P.actf(junk[:, 0:8], xs[b][:, 0:8], AF.Square, [("xs3", b)], ["junk3", "ssd"], accum=ssd[:, 31:32])
            P.ts("dve", ssd[:, t:t + 1], ssd[:, t:t + 1], 1.0 / D, EPS, ALU.mult, ALU.add, ["ssd"], ["ssd"])
            P.actf(ssd[:, t:t + 1], ssd[:, t:t + 1], AF.Ln, ["ssd"], ["ssd"])
            P.actf(ssd[:, t:t + 1], ssd[:, t:t + 1], AF.Exp, ["ssd"], ["ssd"], scale=-0.5)
            P.stt("dve", xs[b], xs[b], ssd[:, t:t + 1], fg, ALU.mult, ALU.mult, [("xs3", b), "ssd", "fg"], [("xs3", b)])
            P.dma("sp", lambda e, t=t, b=b: e.dma_start(out=out_d[t * 128:(t + 1) * 128, :], in_=xs[b]), reads=[("xs3", b)])

    def finish(self):
        self.P.wait_all("sp")
        self.P.emit()
        return self.nc


def rope_tables():
    nf = 16
    inv = (np.float32(10000.0) ** (-np.arange(nf, dtype=np.float32) / np.float32(nf))).astype(np.float32)
    g = np.arange(SEQ)
    row = (g // 64).astype(np.float32)
    col = (g % 64).astype(np.float32)
    ar = row[:, None] * inv
    ac = col[:, None] * inv
    ang = np.concatenate([ar, ar, ac, ac], axis=-1).astype(np.float32)
    cos = np.cos(ang).astype(np.float32)
    sin = np.sin(ang).astype(np.float32)
    sign = np.ones(64, np.float32)
    sign[0:16] = -1
    sign[32:48] = -1
    return cos, sin * sign


def core_inputs_l0(r, x, c, ctx, c_ctx, norm_g, ada_w, ada_b, attn_w_in, cosf, sinf):
    s, e = r * NT, (r + 1) * NT
    xh = np.zeros((256, D), np.float32)
    if r > 0:
        xh[0:128] = x[0, s - 128:s]
    if r < NCORES - 1:
        xh[128:256] = x[0, e:e + 128]
    cc = np.stack([c[0], c_ctx], axis=-1)
    ccT = cc.reshape(8, 128, 2).transpose(1, 0, 2).reshape(128, 16)
    lo, hi = s - 128, e + 128
    idx = np.clip(np.arange(lo, hi), 0, SEQ - 1)
    ct = np.concatenate([cosf[idx].T, cosf[idx].T], axis=0)
    sn = np.concatenate([sinf[idx].T, sinf[idx].T], axis=0)
    flags = np.zeros((128, 4), np.float32)
    flags[:, 0] = 1.0 if r > 0 else 0.0
    flags[:, 1] = 1.0 if r < NCORES - 1 else 0.0
    return {
        "ident": np.eye(128, dtype=np.float32), "flags": flags,
        "ada_w": ada_w, "ada_b": ada_b, "norm_g": norm_g, "ccT": np.ascontiguousarray(ccT),
        "xo": np.ascontiguousarray(x[0, s:e]), "xh": xh, "ctx": np.ascontiguousarray(ctx[0]),
        "w_in0": attn_w_in[0], "rope_cos": np.ascontiguousarray(ct), "rope_sin": np.ascontiguousarray(sn),
    }


def window_masks():
    kk = np.arange(128)[:, None]
    qq = np.arange(128)[None, :]
    m = np.ones((128, 640), np.float32)
    m[:, 0:128] = (kk >= qq)
    m[:, 256:384] = (kk <= qq)
    return m


def extra_inputs_B(attn_w_out, attn_sink, lq1, lk1, lq2, lk2, subln_g):
    sink = attn_sink[0]
    sink_bc = np.zeros((128, 4), np.float32)
    for c in range(4):
        sink_bc[0:64, c] = sink[2 * c]
        sink_bc[64:128, c] = sink[2 * c + 1]
    return {
        "w_out0": attn_w_out[0], "sink_bc": sink_bc, "sink_row": np.ascontiguousarray(sink[None, :]),
        "lamv": np.ascontiguousarray(np.concatenate([lq1[0], lk1[0], lq2[0], lk2[0]])[None, :]),
        "subln": np.ascontiguousarray(subln_g[0][:, None]), "masks": window_masks(),
    }


_cache = {}


def get_prog(stage):
    if stage not in _cache:
        b = Builder(stage)
        b.consts()
        if stage == "A":
            b.l0_phase1()
        elif stage == "B":
            b.l0_phase1()
            b.l0_phase2()
        elif stage == "C":
            b.l1_local()
        elif stage == "D":
            b.l1_final()
        _cache[stage] = (b.finish(), b)
    return _cache[stage]


def core_inputs_l1(r, xl1, xc1, c, c_ctx, norm_g, ada_w, ada_b, rec_w_in, rec_conv_w, rec_conv_b, rec_wa, rec_ba,
                   rec_wx, rec_bx, rec_lam):
    s, e = r * NT, (r + 1) * NT
    halo = np.zeros((128, D), np.float32)
    if r > 0:
        halo[0:2] = xl1[s - 2:s]
    if r < NCORES - 1:
        halo[2] = xl1[e]
    cc = np.stack([c[0], c_ctx], axis=-1)
    ccT = cc.reshape(8, 128, 2).transpose(1, 0, 2).reshape(128, 16)
    flags = np.zeros((128, 4), np.float32)
    flags[:, 0] = 1.0 if r > 0 else 0.0
    flags[:, 1] = 1.0 if r < NCORES - 1 else 0.0
    convw_t = rec_conv_w[0].reshape(4, NB, BD).transpose(2, 1, 0)
    chan = np.zeros((BD, NB, 8), np.float32)
    chan[:, :, 0] = rec_conv_b[0].reshape(NB, BD).T
    for d in range(2):
        chan[:, :, 1 + 3 * d] = rec_ba[0, d].reshape(NB, BD).T
        chan[:, :, 2 + 3 * d] = rec_bx[0, d].reshape(NB, BD).T
        chan[:, :, 3 + 3 * d] = rec_lam[0, d].reshape(NB, BD).T
    wbd = np.stack([rec_wa[0, 0], rec_wx[0, 0], rec_wa[0, 1], rec_wx[0, 1]], axis=0)
    wbd_t = wbd.transpose(2, 1, 0, 3)
    return {
        "ident": np.eye(128, dtype=np.float32), "flags": flags,
        "ada_w": ada_w, "ada_b": ada_b, "norm_g": norm_g, "ccT": np.ascontiguousarray(ccT),
        "xl1_in": np.ascontiguousarray(xl1[s:e]), "xl1_halo": halo, "xc1_in": np.ascontiguousarray(xc1),
        "w_in1": rec_w_in[0], "convw_t": np.ascontiguousarray(convw_t), "chan_t": chan,
        "wbd_t": np.ascontiguousarray(wbd_t),
    }


def run_layer1(inp, xl1, xc1):
    ncC, _ = get_prog("C")
    insC = [core_inputs_l1(r, xl1, xc1, inp["c"], inp["c_ctx"], inp["norm_g"], inp["ada_w"], inp["ada_b"], inp["rec_w_in"],
                           inp["rec_conv_w"], inp["rec_conv_b"], inp["rec_wa"], inp["rec_ba"], inp["rec_wx"], inp["rec_bx"],
                           inp["rec_lam"]) for r in range(NCORES)]
    resC = run_bass_kernel_spmd(ncC, insC, core_ids=list(range(NCORES))).results
    summ_all = np.stack([np.asarray(resC[r]["summ"]) for r in range(NCORES)], axis=0)
    ncD, _ = get_prog("D")
    insD = []
    for r in range(NCORES):
        sel = np.zeros((128, 16), np.float32)
        for j in range(NCORES):
            sel[:, j] = 1.0 if j < r else 0.0
            sel[:, 8 + j] = 1.0 if j > r else 0.0
        cc = np.stack([inp["c"][0], inp["c_ctx"]], axis=-1)
        ccT = cc.reshape(8, 128, 2).transpose(1, 0, 2).reshape(128, 16)
        insD.append({
            "ident": np.eye(128, dtype=np.float32), "flags": insC[r]["flags"],
            "ada_w": inp["ada_w"], "ada_b": inp["ada_b"], "norm_g": inp["norm_g"], "ccT": np.ascontiguousarray(ccT),
            "xl1_in": insC[r]["xl1_in"], "spill_in": np.asarray(resC[r]["spill"]), "summ_all": summ_all, "selmask": sel,
            "w_out1": inp["rec_w_out"][0], "final_g": np.ascontiguousarray(inp["final_g"][None, :]),
        })
    resD = run_bass_kernel_spmd(ncD, insD, core_ids=list(range(NCORES))).results
    return np.concatenate([np.asarray(resD[r]["out"]) for r in range(NCORES)], axis=0)


def run_layer0(inp):
    cosf, sinf = rope_tables()
    ncA, _ = get_prog("A")
    insA = [core_inputs_l0(r, inp["x"], inp["c"], inp["ctx"], inp["c_ctx"], inp["norm_g"], inp["ada_w"], inp["ada_b"],
                           inp["attn_w_in"], cosf, sinf) for r in range(NCORES)]
    resA = run_bass_kernel_spmd(ncA, insA, core_ids=list(range(NCORES))).results
    kt_full = np.concatenate([np.asarray(resA[r]["kt_sh"]) for r in range(NCORES)], axis=0)
    v_full = np.concatenate([np.asarray(resA[r]["v_sh"]) for r in range(NCORES)], axis=0)
    ncB, _ = get_prog("B")
    ex = extra_inputs_B(inp["attn_w_out"], inp["attn_sink"], inp["lam_q1"], inp["lam_k1"], inp["lam_q2"], inp["lam_k2"],
                        inp["subln_g"])
    insB = []
    for r in range(NCORES):
        d_ = dict(insA[r])
        d_.update(ex)
        d_["kt_full"] = kt_full
        d_["v_full"] = v_full
        insB.append(d_)
    resB = run_bass_kernel_spmd(ncB, insB, core_ids=list(range(NCORES))).results
    xl1 = np.concatenate([np.asarray(resB[r]["xl1"]) for r in range(NCORES)], axis=0)
    xc1 = np.asarray(resB[0]["xc1"])
    return xl1, xc1


def kernel(**inputs):
    inp = {k: np.ascontiguousarray(np.asarray(v, dtype=np.float32)) for k, v in inputs.items()}
    xl1, xc1 = run_layer0(inp)
    out = run_layer1(inp, xl1, xc1)
    return out[None].astype(np.float32)
```

```python
import math
import numpy as np
import ml_dtypes
import concourse.bass as bass
import concourse.mybir as mybir
from concourse.bass_utils import run_bass_kernel_spmd

F32 = mybir.dt.float32
BF16 = mybir.dt.bfloat16
AF = mybir.ActivationFunctionType
ALU = mybir.AluOpType
AX = mybir.AxisListType

ENGS = ("pe", "act", "dve", "pool", "sp")
NDSEM = 8

NCORES = 8
D = 1024
SEQ = 16384
NT = SEQ // NCORES
NTILE = NT // 128
CTX = 256
EPS = 1e-6
RW = 1280
NB = 16
BD = 80


class Prog:
    def __init__(self, nc):
        self.nc = nc
        self.lists = {e: [] for e in ENGS}
        self.count = {}
        self.waited = {e: {} for e in ENGS}
        self.last_write = {}
        self.readers = {}
        self.dma_n = {e: 0 for e in ENGS}
        self.dma_events = {e: {} for e in ENGS}
        self.semh = {}
        self.sb_off = 0
        self.arena_bytes = 206 * 1024
        self.arena = nc.alloc_sbuf_tensor("arena", [128, self.arena_bytes], mybir.dt.uint8)
        self.psum = nc.alloc_psum_tensor("psum", [128, 4096], F32)

    def bank(self, b, n=1):
        return self.psum[:, b * 512:(b + n) * 512]

    def sb(self, name, shape, dtype, off=None):
        nbytes = int(np.prod(shape[1:])) * mybir.dt.size(dtype)
        if off is None:
            off = self.sb_off
            self.sb_off = (off + nbytes + 63) // 64 * 64
        assert off + nbytes <= self.arena_bytes, (name, off, nbytes)
        ap = self.arena[:, off:off + nbytes].bitcast(dtype)
        if len(shape) == 3:
            ap = ap.rearrange("p (a b) -> p a b", a=shape[1])
        elif len(shape) == 4:
            ap = ap.rearrange("p (a b c) -> p a b c", a=shape[1], b=shape[2])
        if shape[0] < 128:
            ap = ap[0:shape[0]]
        return ap

    def _need(self, eng, ev, waits):
        if ev is None:
            return
        key, val = ev
        if self.waited[eng].get(key, 0) >= val:
            return
        self.waited[eng][key] = val
        for w in waits:
            if w[0] == key:
                if w[1] < val:
                    w[1] = val
                return
        waits.append([key, val])

    def _deps(self, eng, reads, writes):
        waits = []
        for r in reads:
            self._need(eng, self.last_write.get(r), waits)
        for w in writes:
            self._need(eng, self.last_write.get(w), waits)
            for ev in self.readers.get(w, ()):
                self._need(eng, ev, waits)
        return waits

    def _commit(self, ev, reads, writes):
        for r in reads:
            self.readers.setdefault(r, []).append(ev)
        for w in writes:
            self.last_write[w] = ev
            self.readers[w] = []

    def op(self, eng, fn, reads=(), writes=()):
        waits = self._deps(eng, reads, writes)
        key = ("e", eng)
        self.count[key] = self.count.get(key, 0) + 1
        ev = (key, self.count[key])
        self.lists[eng].append((waits, fn, (key, 1)))
        self._commit(ev, reads, writes)
        return ev

    def dma(self, q, fn, reads=(), writes=()):
        i = self.dma_n[q]
        self.dma_n[q] = i + 1
        waits = self._deps(q, reads, writes)
        if i >= NDSEM:
            self._need(q, self.dma_events[q][i - NDSEM], waits)
        key = ("d", q, i % NDSEM)
        ev = (key, 16 * (i // NDSEM + 1))
        self.dma_events[q][i] = ev
        self.lists[q].append((waits, fn, (key, 16)))
        self._commit(ev, reads, writes)
        return ev

    def cc(self, in_ap, out_ap, reads=(), writes=()):
        q = "pool"
        waits = self._deps(q, reads, writes)
        i = self.count.get(("c", "n"), 0)
        self.count[("c", "n")] = i + 1
        key = ("c", i)
        self.count[key] = 1
        ev = (key, 1)

        def fn(e):
            return e.collective_compute("AllGather", ALU.bypass, replica_groups=[list(range(NCORES))],
                                        ins=[in_ap], outs=[out_ap])
        self.lists[q].append((waits, fn, (key, None)))
        self._commit(ev, reads, writes)
        return ev

    def wait_all(self, eng):
        waits = []
        for key, cnt in list(self.count.items()):
            if key == ("c", "n"):
                continue
            self._need(eng, (key, cnt), waits)
        for q in ENGS:
            n = self.dma_n[q]
            for i in range(max(0, n - NDSEM), n):
                self._need(eng, self.dma_events[q][i], waits)
        self.lists[eng].append((waits, None, None))

    def barrier(self):
        for e in ENGS:
            self.wait_all(e)

    def emit(self):
        nc = self.nc
        keys = set()
        for e in ENGS:
            for waits, fn, sig in self.lists[e]:
                for k, _ in waits:
                    keys.add(k)
                if sig is not None:
                    keys.add(sig[0])
        for k in sorted(keys, key=str):
            self.semh[k] = nc.alloc_semaphore("s_" + "_".join(str(x) for x in k))

        def run(eng_name):
            def body(engine):
                for waits, fn, sig in self.lists[eng_name]:
                    for k, v in waits:
                        engine.wait_ge(self.semh[k], v)
                    if fn is None:
                        continue
                    ins = fn(engine)
                    if sig[1] is None:
                        ins.then_inc(self.semh[sig[0]])
                    else:
                        ins.then_inc(self.semh[sig[0]], sig[1])
            return body

        with nc.Block() as block:
            block.tensor(run("pe"))
            block.scalar(run("act"))
            block.vector(run("dve"))
            block.gpsimd(run("pool"))
            block.sync(run("sp"))

    def ld(self, out, in_, w, r=(), q="sp"):
        return self.dma(q, lambda e: e.dma_start(out=out, in_=in_), reads=r, writes=w)

    def cp(self, eng, out, in_, r, w):
        if eng == "act":
            return self.op("act", lambda e: e.activation(out=out, in_=in_, func=AF.Copy), r, w)
        return self.op(eng, lambda e: e.tensor_copy(out=out, in_=in_), r, w)

    def tt(self, eng, out, a, b, op, r, w):
        return self.op(eng, lambda e: e.tensor_tensor(out=out, in0=a, in1=b, op=op), r, w)

    def ts(self, eng, out, a, s1, s2, op0, op1, r, w):
        if s2 is None:
            return self.op(eng, lambda e: e.tensor_scalar(out=out, in0=a, scalar1=s1, scalar2=None, op0=op0), r, w)
        return self.op(eng, lambda e: e.tensor_scalar(out=out, in0=a, scalar1=s1, scalar2=s2, op0=op0, op1=op1), r, w)

    def stt(self, eng, out, a, s, b, op0, op1, r, w):
        return self.op(eng, lambda e: e.scalar_tensor_tensor(out=out, in0=a, scalar=s, in1=b, op0=op0, op1=op1), r, w)

    def actf(self, out, in_, func, r, w, bias=0.0, scale=1.0, accum=None):
        if accum is None:
            return self.op("act", lambda e: e.activation(out=out, in_=in_, func=func, bias=bias, scale=scale), r, w)
        return self.op("act", lambda e: e.activation(out=out, in_=in_, func=func, bias=bias, scale=scale,
                                                     accum_out=accum), r, w)

    def mm(self, items, r, w):
        def fn(e):
            ins = None
            for (o, l, rh, st, sp) in items:
                ins = e.matmul(o, lhsT=l, rhs=rh, start=st, stop=sp)
            return ins
        return self.op("pe", fn, r, w)

    def tr(self, items, r, w):
        def fn(e):
            ins = None
            for (o, i, idn) in items:
                ins = e.transpose(out=o, in_=i, identity=idn)
            return ins
        return self.op("pe", fn, r, w)


def _dram(nc, name, shape, dtype, kind):
    return nc.dram_tensor(name, list(shape), dtype, kind=kind).ap()


class Builder:
    def __init__(self, stage):
        self.stage = stage
        self.nc = bass.Bass("TRN2", target_bir_lowering=False)
        self.P = Prog(self.nc)
        self.inputs = []
        self.outputs = []

    def din(self, name, shape, dtype=F32):
        self.inputs.append(name)
        return _dram(self.nc, name, shape, dtype, "ExternalInput")

    def dout(self, name, shape, dtype=F32):
        self.outputs.append(name)
        return _dram(self.nc, name, shape, dtype, "ExternalOutput")

    def dint(self, name, shape, dtype=F32):
        return self.nc.dram_tensor(name, list(shape), dtype).ap()

    def dio(self, name, shape, dtype, kind):
        if self.stage == "F":
            if not hasattr(self, "_scratch"):
                self._scratch = {}
            if name not in self._scratch:
                self._scratch[name] = self.dint(name, shape, dtype)
            return self._scratch[name]
        return self.din(name, shape, dtype) if kind == "in" else self.dout(name, shape, dtype)

    def consts(self):
        P = self.P
        ident_d = self.din("ident", [128, 128])
        self.ident_f = P.sb("ident_f", [128, 128], F32)
        self.ident_b = P.sb("ident_b", [128, 128], BF16)
        self.ones_b = P.sb("ones_b", [128, 128], BF16)
        self.ones_f = P.sb("ones_f", [128, 128], F32)
        self.mhalf = P.sb("mhalf", [128, 1], F32)
        P.ld(self.ident_f, ident_d, w=["ident_f"])
        P.cp("dve", self.ident_b, self.ident_f, ["ident_f"], ["ident_b"])
        P.op("pool", lambda e: e.memset(self.ones_b, 1.0), writes=["ones_b"])
        P.op("pool", lambda e: e.memset(self.ones_f, 1.0), writes=["ones_f"])
        P.op("pool", lambda e: e.memset(self.mhalf, -0.5), writes=["mhalf"])
        self.blk_b = P.sb("blk_b", [128, 128], BF16)
        P.op("pool", lambda e: e.memset(self.blk_b, 1.0), writes=["blk_b"])
        P.op("pool", lambda e: e.memset(self.blk_b[0:64, 64:128], 0.0), writes=["blk_b"])
        P.op("pool", lambda e: e.memset(self.blk_b[64:128, 0:64], 0.0), writes=["blk_b"])
        flags_d = self.din("flags", [128, 4])
        self.flags = P.sb("flags", [128, 4], F32)
        P.ld(self.flags, flags_d, w=["flags"])
        self.vec = {}
        for nm in ("gs", "sh", "gt", "gsc", "shc", "gtc"):
            self.vec[nm] = P.sb("vec_" + nm, [128, 1024], F32)
        self.ZA = P.sb_off
        self.ZP = self.ZA + 66 * 1024
        self.ZB = self.ZP + 56 * 1024

    def adaln(self, l):
        P = self.P
        if not hasattr(self, "ada_w_d"):
            self.ada_w_d = self.din("ada_w", [2, 1024, 3072])
            self.ada_b_d = self.din("ada_b", [2, 3072])
            self.norm_g_d = self.din("norm_g", [2, 1024])
            self.ccT_d = self.din("ccT", [128, 16])
            self.sil = P.sb("sil", [128, 16], F32)
            self.silB = P.sb("silB", [128, 16, 128], F32)
            self.adaw = [P.sb("adaw%d" % i, [128, 8, 512], F32) for i in range(2)]
            self.adab = P.sb("adab", [128, 512], F32)
            self.gbc = P.sb("gbc", [128, 1024], F32)
            self.siltmp = P.sb("siltmp", [128, 16], F32)
        P.ld(self.sil, self.ccT_d, w=["sil"])
        tmp = self.siltmp
        P.actf(tmp, self.sil, AF.Exp, ["sil"], ["siltmp"], scale=-1.0)
        P.ts("dve", tmp, tmp, 1.0, None, ALU.add, None, ["siltmp"], ["siltmp"])
        P.op("dve", lambda e: e.reciprocal(out=tmp, in_=tmp), ["siltmp"], ["siltmp"])
        P.tt("dve", self.sil, self.sil, tmp, ALU.mult, ["sil", "siltmp"], ["sil"])
        for j in range(16):
            P.cp("dve", self.silB[:, j, :], self.sil[:, j:j + 1].to_broadcast([128, 128]), ["sil"], [("silB", j)])
        vec = self.vec
        wv = self.ada_w_d[l].rearrange("(k p) n -> p k n", p=128)
        P.ld(self.gbc, self.norm_g_d[l:l + 1, :].to_broadcast([128, 1024]), w=["gbc"])
        order = [("sh", "shc"), ("sh", "shc"), ("gs", "gsc"), ("gs", "gsc"), ("gt", "gtc"), ("gt", "gtc")]
        for n in range(6):
            wb = self.adaw[n % 2]
            wreg = ("adaw", n % 2)
            P.ld(wb, wv[:, :, n * 512:(n + 1) * 512], w=[wreg])
            P.ld(self.adab, self.ada_b_d[l:l + 1, n * 512:(n + 1) * 512].to_broadcast([128, 512]), w=["adab"])
            for v in range(2):
                items = [(P.bank(v), self.silB[:, 2 * k + v, :], wb[:, k, :], k == 0, k == 7) for k in range(8)]
                P.mm(items, [wreg] + [("silB", 2 * k + v) for k in range(8)], [("ps", v)])
                dst = vec[order[n][v]][:, (n % 2) * 512:(n % 2 + 1) * 512]
                dreg = ("vec", order[n][v], n % 2)
                P.tt("dve", dst, P.bank(v), self.adab, ALU.add, [("ps", v), "adab"], [dreg])
                if n in (2, 3):
                    P.stt("dve", dst, dst, 1.0, self.gbc[:, (n % 2) * 512:(n % 2 + 1) * 512], ALU.add, ALU.mult,
                          [dreg, "gbc"], [dreg])

    def vreg(self, nm):
        return [("vec", nm, 0), ("vec", nm, 1)]

    def norm_all(self, srcs, kinds, hlT):
        P = self.P
        n = len(srcs)
        ssa = self.ss_all
        junk = self.junk
        for i, src in enumerate(srcs):
            b = i % 2
            xs = self.xs[b]
            P.ld(xs, src, w=[("xs", b)])
            P.actf(junk, xs, AF.Square, [("xs", b)], ["junk", "ss_all"], accum=ssa[:, i:i + 1])
        P.actf(junk[:, 0:8], self.xs[(n - 1) % 2][:, 0:8], AF.Square, [("xs", (n - 1) % 2)], ["junk", "ss_all"],
               accum=ssa[:, 31:32])
        P.ts("dve", ssa[:, 0:n], ssa[:, 0:n], 1.0 / D, EPS, ALU.mult, ALU.add, ["ss_all"], ["ss_all"])
        P.actf(ssa[:, 0:n], ssa[:, 0:n], AF.Ln, ["ss_all"], ["ss_all"])
        P.actf(ssa[:, 0:n], ssa[:, 0:n], AF.Exp, ["ss_all"], ["ss_all"], scale=-0.5)
        for i, src in enumerate(srcs):
            b = i % 2
            gs, sh = kinds[i]
            xs = self.xs[b]
            P.ld(xs, src, w=[("xs", b)])
            tmp = self.ntmp
            P.stt("dve", tmp, xs, ssa[:, i:i + 1], self.vec[gs], ALU.mult, ALU.mult,
                  [("xs", b), "ss_all"] + self.vreg(gs), ["ntmp"])
            hlb = self.hlb[b]
            P.tt("pool", hlb, tmp, self.vec[sh], ALU.add, ["ntmp"] + self.vreg(sh), [("hlb", b)])
            pst = P.bank(2 + b).bitcast(BF16)
            P.tr([(pst[:, k * 128:(k + 1) * 128], hlb[:, k * 128:(k + 1) * 128], self.ident_b) for k in range(8)],
                 [("hlb", b), "ident_b"], [("ps", 2 + b)])
            P.cp("dve", hlT[:, :, i * 128:(i + 1) * 128], pst.rearrange("p (k t) -> p k t", k=8),
                 [("ps", 2 + b)], [("hlT", i)])

    def norm_bufs(self):
        P = self.P
        self.xs = [P.sb("xs%d" % i, [128, 1024], F32) for i in range(2)]
        self.junk = P.sb("junk", [128, 1024], BF16)
        self.ss_all = P.sb("ss_all", [128, 32], F32)
        self.ntmp = P.sb("ntmp", [128, 1024], F32)
        self.hlb = [P.sb("hlb%d" % i, [128, 1024], BF16) for i in range(2)]

    def l0_phase1(self):
        P = self.P
        st = self.stage
        full = st != "A"
        xo_d = self.din("xo", [NT, D])
        xh_d = self.din("xh", [256, D])
        ctx_d = self.din("ctx", [CTX, D])
        self.xo_d, self.ctx_d = xo_d, ctx_d
        w_in_d = self.din("w_in0", [D, 3328])
        cos_d = self.din("rope_cos", [128, 2304])
        sin_d = self.din("rope_sin", [128, 2304])
        wv = w_in_d.rearrange("(k p) n -> p k n", p=128)

        import os
        dbg = int(os.environ.get("DBG_STOP", "99"))
        P.sb_off = self.ZA
        if st in ("A", "F"):
            self.kt_sh = self.dio("kt_sh", [512, NT], BF16, "out")
            self.v_sh = self.dio("v_sh", [512, NT], BF16, "out")
        if dbg <= 0:
            return
        self.adaln(0)
        P.barrier()
        if dbg <= 1:
            return
        P.sb_off = self.ZA
        NCOL = 20 * 128
        hlT = P.sb("hlT", [128, 8, NCOL], BF16)
        wst = [P.sb("wst%d" % i, [128, 8, 128], F32) for i in range(2)]
        wbf = [P.sb("wbf%d" % i, [128, 8, 256], BF16) for i in range(2)]
        t1 = P.sb("rp_t1", [128, 512], F32)
        t2 = P.sb("rp_t2", [128, 512], F32)
        sqb = P.sb("rp_sq", [128, 512], BF16)
        kst = [P.sb("kst%d" % i, [128, 512], BF16) for i in range(2)]
        smax = P.sb("rp_smax", [128, 1], F32)
        assert P.sb_off <= self.ZP, P.sb_off

        P.sb_off = self.ZP
        self.QAT = P.sb("QAT", [128, 4, 2304], BF16)
        self.KAT = P.sb("KAT", [128, 2, 2560], BF16)
        self.VA = P.sb("VA", [128, 20, 128], BF16)
        self.QBT = P.sb("QBT", [128, 4, 2304], BF16)
        self.KBTc = P.sb("KBTc", [128, 4, 256], BF16)
        self.VBc = P.sb("VBc", [128, 2, 512], BF16)
        self.stat = P.sb("stat", [128, 4], F32)
        assert P.sb_off <= self.ZB, P.sb_off
        P.op("pool", lambda e: e.memset(self.stat, 0.0), writes=["stat"])

        P.sb_off = self.ZB
        self.norm_bufs()
        cosT = P.sb("cosT", [128, 2304], F32)
        sinT = P.sb("sinT", [128, 2304], F32)
        wva_st = P.sb("wva_st", [128, 8, 128], F32)
        wva = P.sb("wva", [128, 8, 128], BF16)
        wvb = P.sb("wvb", [128, 8, 512], BF16)
        wvb_st = P.sb("wvb_st", [128, 2, 512], F32)
        vst = [P.sb("vst%d" % i, [128, 512], BF16) for i in range(2)]

        srcs = [xh_d[0:128, :]] + [xo_d[t * 128:(t + 1) * 128, :] for t in range(NTILE)] + [xh_d[128:256, :]] + \
               [ctx_d[0:128, :], ctx_d[128:256, :]]
        kinds = [("gs", "sh")] * 18 + [("gsc", "shc")] * 2
        self.norm_all(srcs, kinds, hlT)
        P.ld(cosT, cos_d, w=["cosT"])
        P.ld(sinT, sin_d, w=["sinT"])

        if dbg <= 2:
            return
        perm = [(16, 0), (0, 16), (48, 32), (32, 48)]
        self.chunk_i = 0

        def load_chunk(colspec, roped):
            i = self.chunk_i
            self.chunk_i += 1
            b = i % 2
            off = 0
            for (c0, wdt) in colspec:
                P.ld(wst[b][:, :, off:off + wdt], wv[:, :, c0:c0 + wdt], w=[("wst", b)])
                off += wdt
            P.cp("act", wbf[b][:, :, 0:128], wst[b], [("wst", b)], [("wbf", b)])
            if roped:
                src4 = wst[b].rearrange("p k (b d) -> p k b d", b=2)
                dst4 = wbf[b][:, :, 128:256].rearrange("p k (b d) -> p k b d", b=2)
                for (so, do) in perm:
                    P.cp("dve", dst4[:, :, :, do:do + 16], src4[:, :, :, so:so + 16], [("wst", b)], [("wbf", b)])
            return b

        def fm_group(b, c0, n, rc, gi, dst, dreg, statcol=None, silu=False):
            pb = 4 + 2 * (gi % 2)
            hregs = [("hlT", t) for t in range(c0 // 128, (c0 + n) // 128)]
            items = [(P.bank(pb)[:, 0:n], wbf[b][:, k, 0:128], hlT[:, k, c0:c0 + n], k == 0, k == 7) for k in range(8)]
            P.mm(items, [("wbf", b)] + hregs, [("ps", pb)])
            if rc is not None:
                items = [(P.bank(pb + 1)[:, 0:n], wbf[b][:, k, 128:256], hlT[:, k, c0:c0 + n], k == 0, k == 7) for k in range(8)]
                P.mm(items, [("wbf", b)] + hregs, [("ps", pb + 1)])
                P.tt("dve", t1[:, 0:n], P.bank(pb)[:, 0:n], cosT[:, rc:rc + n], ALU.mult, [("ps", pb), "cosT"], ["rp_t1"])
                P.tt("dve", t2[:, 0:n], P.bank(pb + 1)[:, 0:n], sinT[:, rc:rc + n], ALU.mult, [("ps", pb + 1), "sinT"], ["rp_t2"])
                P.tt("pool", dst, t1[:, 0:n], t2[:, 0:n], ALU.add, ["rp_t1", "rp_t2"], [dreg])
            elif silu:
                P.actf(dst, P.bank(pb)[:, 0:n], AF.Silu, [("ps", pb)], [dreg])
            else:
                P.cp("act", dst, P.bank(pb)[:, 0:n], [("ps", pb)], [dreg])
            if statcol is not None:
                P.tt("pool", sqb[:, 0:n], dst, dst, ALU.mult, [dreg], ["rp_sq"])
                P.mm([(P.bank(pb + 1)[:, 0:n], self.blk_b, sqb[:, 0:n], True, True)],
                     ["rp_sq", "blk_b"], [("ps", pb + 1)])
                P.op("dve", lambda e: e.tensor_reduce(out=smax, in_=P.bank(pb + 1)[:, 0:n], axis=AX.X, op=ALU.max),
                     [("ps", pb + 1)], ["rp_smax"])
                P.tt("dve", self.stat[:, statcol:statcol + 1], self.stat[:, statcol:statcol + 1], smax, ALU.max,
                     ["rp_smax", "stat"], ["stat"])

        own_groups = [(128 + 512 * g, 512, 128 + 512 * g) for g in range(4)] + [(2304, 256, None)]
        ka_groups = [(512 * g, 512, 512 * g) for g in range(4)] + [(2048, 256, 2048), (2304, 256, None)]

        def own_dst(T, c, nm, gi):
            if gi < 4:
                return T[:, c, 512 * gi:512 * gi + 512], (nm, c, gi)
            return T[:, c, 2048:2304], (nm, c, 4)

        P.ld(wva_st, wv[:, :, 640:768], w=["wva_st"])
        P.cp("pool", wva, wva_st, ["wva_st"], ["wva"])
        for k2 in range(4):
            P.ld(wvb_st, wv[:, 2 * k2:2 * k2 + 2, 1792:2304], w=["wvb_st"])
            P.cp("pool", wvb[:, 2 * k2:2 * k2 + 2, :], wvb_st, ["wvb_st"], [("wvb", k2)])
        for i in range(20):
            c0 = i * 128
            if full:
                items = [(P.bank(4)[:, 0:128], hlT[:, k, c0:c0 + 128], wva[:, k, :], k == 0, k == 7) for k in range(8)]
                P.mm(items, [("hlT", i), "wva"], [("ps", 4)])
                P.cp("act", self.VA[:, i, :], P.bank(4)[:, 0:128], [("ps", 4)], [("VA", i)])
            own = 1 <= i <= 16
            if (own and st in ("A", "F")) or (i >= 18 and full):
                items = [(P.bank(5), hlT[:, k, c0:c0 + 128], wvb[:, k, :], k == 0, k == 7) for k in range(8)]
                P.mm(items, [("hlT", i)] + [("wvb", k2) for k2 in range(4)], [("ps", 5)])
                if i >= 18:
                    P.cp("act", self.VBc[:, i - 18, :], P.bank(5), [("ps", 5)], [("VBc", i - 18)])
                else:
                    j = i % 2
                    t = i - 1
                    P.cp("act", vst[j], P.bank(5), [("ps", 5)], [("vst", j)])
                    dstv = self.v_sh.rearrange("(h p) (t d) -> p h t d", p=128, d=128)[:, :, t, :]
                    P.dma("sp", lambda e, j=j, dstv=dstv: e.dma_start(out=dstv, in_=vst[j].rearrange("p (h d) -> p h d", h=4)),
                          reads=[("vst", j)])

        if dbg <= 3:
            return
        if full:
            for c in range(4):
                b = load_chunk([(128 * c, 128)], True)
                for gi, (c0, n, rc) in enumerate(own_groups):
                    dst, dreg = own_dst(self.QAT, c, "QAT", gi)
                    fm_group(b, c0, n, rc, gi, dst, dreg, statcol=0)
            for kv in range(2):
                b = load_chunk([(512 + 64 * kv, 64), (512 + 64 * kv, 64)], True)
                for gi, (c0, n, rc) in enumerate(ka_groups):
                    fm_group(b, c0, n, rc, gi, self.KAT[:, kv, c0:c0 + n], ("KAT", kv, gi), statcol=1)
            for c in range(4):
                b = load_chunk([(768 + 128 * c, 128)], True)
                for gi, (c0, n, rc) in enumerate(own_groups):
                    dst, dreg = own_dst(self.QBT, c, "QBT", gi)
                    fm_group(b, c0, n, rc, gi, dst, dreg, statcol=2)
        kst_i = 0
        for c in range(4):
            b = load_chunk([(1280 + 128 * c, 128)], True)
            for gi, (c0, n, rc) in enumerate(own_groups):
                if gi < 4:
                    j = kst_i % 2
                    kst_i += 1
                    fm_group(b, c0, n, rc, gi, kst[j], ("kst", j), statcol=3)
                    if st in ("A", "F"):
                        P.dma("sp", lambda e, j=j, c=c, gi=gi: e.dma_start(
                            out=self.kt_sh[c * 128:(c + 1) * 128, gi * 512:(gi + 1) * 512], in_=kst[j]), reads=[("kst", j)])
                elif full:
                    fm_group(b, c0, n, rc, gi, self.KBTc[:, c, :], ("KBTc", c), statcol=3)
        if not full:
            return
        P.barrier()
        if st == "F":
            self.kt_full = self.dint("kt_full", [4096, NT], BF16)
            self.v_full = self.dint("v_full", [4096, NT], BF16)
            P.cc(self.kt_sh, self.kt_full, writes=["kt_full"])
            P.cc(self.v_sh, self.v_full, writes=["v_full"])
        P.sb_off = self.ZB
        self.GT = P.sb("GT", [128, 8, 2304], BF16)
        self.ZB2 = P.sb_off
        for c in range(8):
            b = load_chunk([(2304 + 128 * c, 128)], False)
            for gi, (c0, n, rc) in enumerate(own_groups):
                dst, dreg = own_dst(self.GT, c, "GT", gi)
                fm_group(b, c0, n, None, gi, dst, dreg, silu=True)
        P.barrier()

    def l0_phase2(self):
        P = self.P
        if self.stage == "F":
            kt_full, v_full = self.kt_full, self.v_full
        else:
            kt_full = self.din("kt_full", [4096, NT], BF16)
            v_full = self.din("v_full", [4096, NT], BF16)
        w_out_d = self.din("w_out0", [D, D])
        sink_bc_d = self.din("sink_bc", [128, 4])
        sink_row_d = self.din("sink_row", [1, 8])
        lamv_d = self.din("lamv", [1, 256])
        subln_d = self.din("subln", [128, 1])
        masks_d = self.din("masks", [128, 640])
        xl1_d = self.dio("xl1", [NT, D], F32, "out")
        xc1_d = self.dio("xc1", [CTX, D], F32, "out")
        GT = self.GT

        P.sb_off = self.ZA
        KTq = [P.sb("KTq%d" % i, [128, 4096], BF16) for i in range(2)]
        Vq = [P.sb("Vq%d" % i, [128, 32, 128], BF16) for i in range(2)]
        f1 = P.sb("f1", [128, 512], F32)
        f2 = P.sb("f2", [128, 512], F32)
        sqb = P.sb("sqb2", [128, 512], BF16)
        Pb = [P.sb("Pb%d" % i, [128, 1024], BF16) for i in range(2)]
        PA = [P.sb("PA%d" % i, [128, 640], BF16) for i in range(2)]
        PB = [P.sb("PB%d" % i, [128, 640], BF16) for i in range(2)]
        rl = P.sb("rl", [128, 128], F32)
        ot = P.sb("ot", [128, 128], F32)
        mask_f = P.sb("mask_f", [128, 640], F32)
        masks = [P.sb("mask%d" % i, [128, 640], BF16) for i in range(3)]
        xs = [P.sb("xs2_%d" % i, [128, 1024], F32) for i in range(2)]
        small = P.sb("small", [128, 64], F32)
        row = P.sb("rowsc", [1, 512], F32)
        lamv = P.sb("lamv", [1, 256], F32)
        sinkr = P.sb("sinkr", [1, 8], F32)
        sink_bc = P.sb("sink_bc", [128, 4], F32)
        gsub = P.sb("gsub", [128, 1], F32)
        assert P.sb_off <= self.ZP, P.sb_off
        P.sb_off = self.ZB2
        Wout = P.sb("Wout", [128, 8, 1024], BF16)
        wo_st = P.sb("wo_st", [128, 1024], F32)
        w_out_v = w_out_d.rearrange("(k p) n -> p k n", p=128)
        for k in range(8):
            P.ld(wo_st, w_out_v[:, k, :], w=["wo_st"])
            P.cp("pool", Wout[:, k, :], wo_st, ["wo_st"], [("Wout", k)])

        negc = small[:, 0:3]
        esink = small[:, 4:8]
        P.ld(lamv, lamv_d, w=["lamv"])
        P.ld(sinkr, sink_row_d, w=["sinkr"])
        P.ld(sink_bc, sink_bc_d, w=["sink_bc"])
        P.ld(gsub, subln_d, w=["gsub"])
        P.ts("dve", gsub, gsub, 0.8, None, ALU.mult, None, ["gsub"], ["gsub"])
        P.ld(mask_f, masks_d, w=["mask_f"])
        for i in range(3):
            P.cp("dve", masks[i], mask_f, ["mask_f"], [("mask", i)])
        P.ts("dve", masks[0][:, 0:128], masks[0][:, 0:128], self.flags[:, 0:1], None, ALU.mult, None,
             [("mask", 0), "flags"], [("mask", 0)])
        P.ts("dve", masks[2][:, 256:384], masks[2][:, 256:384], self.flags[:, 1:2], None, ALU.mult, None,
             [("mask", 2), "flags"], [("mask", 2)])
        P.tr([(P.bank(0)[0:1, j * 128:(j + 1) * 128], self.stat[:, j:j + 1], self.ident_f) for j in range(4)],
             ["stat", "ident_f"], [("ps", 0)])
        P.op("dve", lambda e: e.tensor_reduce(out=row[:, 0:4], in_=P.bank(0)[0:1, :].rearrange("p (a b) -> p a b", a=4),
                                              axis=AX.X, op=ALU.max), [("ps", 0)], ["row"])
        P.tt("dve", row[:, 8:9], row[:, 0:1], row[:, 1:2], ALU.mult, ["row"], ["row"])
        P.tt("dve", row[:, 9:10], row[:, 2:3], row[:, 3:4], ALU.mult, ["row"], ["row"])
        P.actf(row[:, 8:10], row[:, 8:10], AF.Ln, ["row"], ["row"])
        P.actf(row[:, 8:10], row[:, 8:10], AF.Exp, ["row"], ["row"], scale=0.5, bias=math.log(1.5 / 8.0))
        P.op("dve", lambda e: e.tensor_reduce(out=row[:, 10:11], in_=sinkr, axis=AX.X, op=ALU.max), ["sinkr"], ["row"])
        P.tt("dve", row[:, 8:9], row[:, 8:9], row[:, 10:11], ALU.max, ["row"], ["row"])
        P.tt("dve", lamv[:, 0:64], lamv[:, 0:64], lamv[:, 64:128], ALU.mult, ["lamv"], ["lamv"])
        P.tt("dve", lamv[:, 128:192], lamv[:, 128:192], lamv[:, 192:256], ALU.mult, ["lamv"], ["lamv"])
        P.op("dve", lambda e: e.tensor_reduce(out=row[:, 12:13], in_=lamv[:, 0:64], axis=AX.X, op=ALU.add), ["lamv"], ["row"])
        P.op("dve", lambda e: e.tensor_reduce(out=row[:, 13:14], in_=lamv[:, 128:192], axis=AX.X, op=ALU.add), ["lamv"], ["row"])
        P.actf(row[:, 12:14], row[:, 12:14], AF.Exp, ["row"], ["row"])
        P.tt("dve", row[:, 14:15], row[:, 12:13], row[:, 13:14], ALU.subtract, ["row"], ["row"])
        P.ts("dve", row[:, 16:18], row[:, 8:10], -1.0, None, ALU.mult, None, ["row"], ["row"])
        P.ts("dve", row[:, 18:19], row[:, 14:15], -1.0, -0.2, ALU.mult, ALU.add, ["row"], ["row"])
        P.mm([(P.bank(1)[:, 0:3], self.ones_f[0:1, 0:128], row[0:1, 16:19], True, True)], ["row", "ones_f"], [("ps", 1)])
        P.cp("dve", negc, P.bank(1)[:, 0:3], [("ps", 1)], ["negc"])
        P.actf(esink, sink_bc, AF.Exp, ["sink_bc", "negc"], ["esink"], bias=negc[:, 0:1])

        import os
        ph2 = int(os.environ.get("PH2_STOP", "99"))
        if ph2 <= 1:
            return
        def kat_reg(kv, ti):
            return ("KAT", kv, ti // 4 if ti < 16 else (4 if ti < 18 else 5))

        wi = 0
        for qt in range(18):
            own = qt < 16
            tiles = [qt, qt + 1, qt + 2, 18, 19] if own else [18, 19]
            nk = len(tiles)
            W = nk * 128
            qc0 = qt * 128 if own else 2048 + (qt - 16) * 128
            gi = qt // 4 if own else 4
            mk = None if not own else (0 if qt == 0 else (2 if qt == 15 else 1))
            for c in range(4):
                kv = c // 2
                b = wi % 2
                wi += 1
                items = []
                for hb in range(2):
                    base = 64 * hb
                    for j, ti in enumerate(tiles):
                        items.append((P.psum[:, hb * 1024 + j * 128: hb * 1024 + (j + 1) * 128],
                                      self.KAT[base:base + 64, kv, ti * 128:(ti + 1) * 128],
                                      self.QAT[base:base + 64, c, qc0:qc0 + 128], True, True))
                kregs = sorted(set(kat_reg(kv, ti) for ti in tiles))
                P.mm(items, [("QAT", c, gi)] + kregs, [("ps", 0), ("ps", 1), ("ps", 2), ("ps", 3)])
                P.actf(PA[b][:, 0:W], P.psum[:, 0:W], AF.Exp, [("ps", 0), ("ps", 1), "negc"], [("PA", b)],
                       bias=negc[:, 0:1], scale=0.125)
                P.actf(PB[b][:, 0:W], P.psum[:, 1024:1024 + W], AF.Exp, [("ps", 2), ("ps", 3), "negc"], [("PB", b)],
                       bias=negc[:, 0:1], scale=0.125)
                if mk is not None:
                    P.tt("dve", PA[b][:, 0:W], PA[b][:, 0:W], masks[mk], ALU.mult, [("PA", b), ("mask", mk)], [("PA", b)])
                    P.tt("dve", PB[b][:, 0:W], PB[b][:, 0:W], masks[mk], ALU.mult, [("PB", b), ("mask", mk)], [("PB", b)])
                ob = 4 + b
                OW = P.bank(ob)
                items = []
                for hb, PP in ((0, PA[b]), (1, PB[b])):
                    lo = 64 * hb
                    for j, ti in enumerate(tiles):
                        items.append((OW[lo:lo + 64, 0:128], self.VA[:, ti, kv * 64:(kv + 1) * 64], PP[:, j * 128:(j + 1) * 128],
                                      j == 0, j == nk - 1))
                    for j, ti in enumerate(tiles):
                        items.append((OW[lo:lo + 64, 128:256], self.ones_b[:, 0:64], PP[:, j * 128:(j + 1) * 128],
                                      j == 0, j == nk - 1))
                P.mm(items, [("PA", b), ("PB", b), "ones_b"] + [("VA", ti) for ti in tiles], [("ps", ob)])
                P.ts("dve", rl, OW[:, 128:256], esink[:, c:c + 1], None, ALU.add, None, [("ps", ob), "esink"], ["rl"])
                P.op("dve", lambda e: e.reciprocal(out=rl, in_=rl), ["rl"], ["rl"])
                P.tt("dve", ot, OW[:, 0:128], rl, ALU.mult, [("ps", ob), "rl"], ["ot"])
                P.tt("dve", GT[:, c, qc0:qc0 + 128], ot, GT[:, c, qc0:qc0 + 128], ALU.mult, ["ot", ("GT", c, gi)], [("GT", c, gi)])

        if ph2 <= 2:
            return
        segs = [(h, g, qq) for h in range(4) for g in range(4) for qq in range(4)]

        def load_seg(si):
            h, g, qq = segs[si]
            b = si % 2
            for rr in range(2):
                r = 2 * qq + rr
                P.ld(KTq[b][:, rr * 2048:(rr + 1) * 2048], kt_full[r * 512 + h * 128: r * 512 + (h + 1) * 128, :], w=[("KTq", b)],
                     r=["kt_full"])
                P.ld(Vq[b][:, rr * 16:(rr + 1) * 16, :],
                     v_full[r * 512 + h * 128: r * 512 + (h + 1) * 128, :].rearrange("p (t d) -> p t d", d=128), w=[("Vq", b)],
                     r=["v_full"])

        load_seg(0)
        load_seg(1)
        self.kidx = 0

        def attend(h, qcols, N, qreg, klist):
            n = len(klist)

            def qk(k):
                sb_ = (self.kidx + k) % 2
                kt_ap, v_ap, regs, _ = klist[k]
                base = sb_ * 1024
                P.mm([(P.psum[:, base:base + N], kt_ap[0:64, :], self.QBT[0:64, h, qcols], True, True),
                      (P.psum[:, base + 512:base + 512 + N], kt_ap[64:128, :], self.QBT[64:128, h, qcols], True, True)],
                     [qreg] + regs, [("ps", 2 * sb_), ("ps", 2 * sb_ + 1)])

            qk(0)
            if n > 1:
                qk(1)
            for k in range(n):
                sb_ = (self.kidx + k) % 2
                kt_ap, v_ap, regs, after = klist[k]
                base = sb_ * 1024
                src = P.psum[:, base:base + 1024].rearrange("p (a b) -> p a b", a=2)[:, :, 0:N]
                dst = Pb[sb_].rearrange("p (a b) -> p a b", a=2)[:, :, 0:N]
                P.actf(dst, src, AF.Exp, [("ps", 2 * sb_), ("ps", 2 * sb_ + 1), "negc"], [("Pb", sb_)],
                       bias=negc[:, 1:2], scale=0.125)
                if k + 2 < n:
                    qk(k + 2)
                st_, sp_ = (k == 0), (k == n - 1)
                P.mm([(P.bank(4)[:, 0:N], v_ap, Pb[sb_][:, 0:N], st_, sp_),
                      (P.bank(6)[:, 0:N], self.ones_b, Pb[sb_][:, 0:N], st_, sp_),
                      (P.bank(5)[:, 0:N], v_ap, Pb[sb_][:, 512:512 + N], st_, sp_),
                      (P.bank(7)[:, 0:N], self.ones_b, Pb[sb_][:, 512:512 + N], st_, sp_)],
                     [("Pb", sb_), "ones_b"] + regs, [("ps", 4), ("ps", 5), ("ps", 6), ("ps", 7)])
                if after is not None:
                    after()
            self.kidx += n

        def finalize(h, zc0, N, gi):
            O1, O2, L1, L2 = P.bank(4)[:, 0:N], P.bank(5)[:, 0:N], P.bank(6)[:, 0:N], P.bank(7)[:, 0:N]
            a, b_ = f1[:, 0:N], f2[:, 0:N]
            P.op("dve", lambda e: e.reciprocal(out=a, in_=L1), [("ps", 6)], ["f1"])
            P.tt("dve", a, O1, a, ALU.mult, [("ps", 4), "f1"], ["f1"])
            P.op("dve", lambda e: e.reciprocal(out=b_, in_=L2), [("ps", 7)], ["f2"])
            P.tt("dve", b_, O2, b_, ALU.mult, [("ps", 5), "f2"], ["f2"])
            P.stt("dve", a, b_, negc[:, 2:3], a, ALU.mult, ALU.add, ["f1", "f2", "negc"], ["f1"])
            P.tt("pool", sqb[:, 0:N], a, a, ALU.mult, ["f1"], ["sqb2"])
            P.mm([(P.bank(6)[:, 0:N], self.ones_b, sqb[:, 0:N], True, True)], ["sqb2", "ones_b"], [("ps", 6)])
            P.ts("dve", b_, P.bank(6)[:, 0:N], 1.0 / 128.0, EPS, ALU.mult, ALU.add, [("ps", 6)], ["f2"])
            P.actf(b_, b_, AF.Ln, ["f2"], ["f2"])
            P.actf(b_, b_, AF.Exp, ["f2"], ["f2"], scale=-0.5)
            P.tt("dve", a, a, b_, ALU.mult, ["f1", "f2"], ["f1"])
            P.stt("dve", GT[:, 4 + h, zc0:zc0 + N], a, gsub[:, 0:1], GT[:, 4 + h, zc0:zc0 + N], ALU.mult, ALU.mult,
                  ["f1", "gsub", ("GT", 4 + h, gi)], [("GT", 4 + h, gi)])

        si = 0
        for h in range(4):
            for g in range(4):
                klist = []
                for qq in range(4):
                    b = si % 2
                    for t in range(32):
                        after = None
                        if t == 31:
                            def after(si=si):
                                if si + 2 < len(segs):
                                    load_seg(si + 2)
                        klist.append((KTq[b][:, t * 128:(t + 1) * 128], Vq[b][:, t, :], [("KTq", b), ("Vq", b)], after))
                    si += 1
                for j in range(2):
                    klist.append((self.KBTc[:, h, j * 128:(j + 1) * 128], self.VBc[:, j, h * 128:(h + 1) * 128],
                                  [("KBTc", h), ("VBc", j)], None))
                attend(h, slice(512 * g, 512 * g + 512), 512, ("QBT", h, g), klist)
                finalize(h, 512 * g, 512, g)
            klist = [(self.KBTc[:, h, j * 128:(j + 1) * 128], self.VBc[:, j, h * 128:(h + 1) * 128],
                      [("KBTc", h), ("VBc", j)], None) for j in range(2)]
            attend(h, slice(2048, 2304), 256, ("QBT", h, 4), klist)
            finalize(h, 2048, 256, 4)

        if ph2 <= 3:
            return
        for t in range(18):
            own = t < 16
            c0 = t * 128 if own else 2048 + (t - 16) * 128
            gi = t // 4 if own else 4
            b = t % 2
            src = self.xo_d[t * 128:(t + 1) * 128, :] if own else self.ctx_d[(t - 16) * 128:(t - 15) * 128, :]
            dstd = xl1_d[t * 128:(t + 1) * 128, :] if own else xc1_d[(t - 16) * 128:(t - 15) * 128, :]
            gt = self.vec["gt" if own else "gtc"]
            gtn = "gt" if own else "gtc"
            P.ld(xs[b], src, w=[("xs2", b)])
            for half in range(2):
                pb = 2 * b + half
                P.mm([(P.bank(pb), GT[:, c, c0:c0 + 128], Wout[:, c, half * 512:(half + 1) * 512], c == 0, c == 7) for c in range(8)],
                     [("GT", c, gi) for c in range(8)] + [("Wout", k) for k in range(8)], [("ps", pb)])
                P.tt("dve", f1, P.bank(pb), gt[:, half * 512:(half + 1) * 512], ALU.mult, [("ps", pb), ("vec", gtn, half)], ["f1"])
                P.tt("dve", xs[b][:, half * 512:(half + 1) * 512], xs[b][:, half * 512:(half + 1) * 512], f1, ALU.add,
                     ["f1", ("xs2", b)], [("xs2", b)])
            P.dma("sp", lambda e, dstd=dstd, b=b: e.dma_start(out=dstd, in_=xs[b]), reads=[("xs2", b)])

    def halo_exchange(self, xl1_d):
        P = self.P
        P.barrier()
        selT_d = self.din("selT", [24, 128])
        bnd_sh = self.dint("bnd_sh", [3, D])
        bnd_all = self.dint("bnd_all", [24, D])
        halo_d = self.dint("xl1_halo_i", [128, D])
        P.dma("sp", lambda e: e.dma_start(out=bnd_sh[0:1, :], in_=xl1_d[0:1, :]), writes=["bnd_sh"])
        P.dma("sp", lambda e: e.dma_start(out=bnd_sh[1:3, :], in_=xl1_d[NT - 2:NT, :]), writes=["bnd_sh"])
        P.cc(bnd_sh, bnd_all, reads=["bnd_sh"], writes=["bnd_all"])
        P.sb_off = self.ZA
        selT = P.sb("selT", [24, 128], F32)
        bnd = P.sb("bnd", [24, D], F32)
        halo = P.sb("halo", [128, D], F32)
        P.ld(selT, selT_d, w=["selT"])
        P.ld(bnd, bnd_all, w=["bnd"], r=["bnd_all"])
        for half in range(2):
            P.mm([(P.bank(half), selT, bnd[:, half * 512:(half + 1) * 512], True, True)], ["selT", "bnd"], [("ps", half)])
            P.cp("dve", halo[:, half * 512:(half + 1) * 512], P.bank(half), [("ps", half)], ["halo"])
        P.dma("sp", lambda e: e.dma_start(out=halo_d, in_=halo), reads=["halo"], writes=["halo_d"])
        P.barrier()
        return halo_d

    def l1_local(self):
        P = self.P
        if self.stage == "F":
            xl1_d = self._scratch["xl1"]
            xc1_d = self._scratch["xc1"]
            xl1h_d = self.halo_exchange(xl1_d)
        else:
            xl1_d = self.din("xl1_in", [NT, D])
            xl1h_d = self.din("xl1_halo", [128, D])
            xc1_d = self.din("xc1_in", [CTX, D])
        w_in_d = self.din("w_in1", [D, 2 * RW])
        convw_d = self.din("convw_t", [BD, NB, 4])
        chan_d = self.din("chan_t", [BD, NB, 8])
        wbd_d = self.din("wbd_t", [BD, NB, 4, BD])
        spill_d = self.dio("spill", [NB, 3, BD, NT], F32, "out")
        summ_d = self.dio("summ", [BD, NB * 6], F32, "out")
        wv = w_in_d.rearrange("(k p) n -> p k n", p=128)

        P.sb_off = self.ZA
        self.adaln(1)
        P.barrier()
        P.sb_off = self.ZA
        NCOL = 19 * 128
        hlT = P.sb("hl1T", [128, 8, NCOL], BF16)
        self.norm_bufs()
        srcs = [xl1_d[t * 128:(t + 1) * 128, :] for t in range(NTILE)] + [xl1h_d] + [xc1_d[0:128, :], xc1_d[128:256, :]]
        kinds = [("gs", "sh")] * 17 + [("gsc", "shc")] * 2
        self.norm_all(srcs, kinds, hlT)
        P.barrier()
        P.sb_off = self.ZA + 8 * NCOL * 2
        P.sb_off = (P.sb_off + 63) // 64 * 64

        convw = P.sb("convw", [BD, NB, 4], F32)
        chan = P.sb("chan", [BD, NB, 8], F32)
        cs = P.sb("cs", [BD, NB, 2], F32)
        summ = P.sb("summ", [BD, NB, 6], F32)
        P.ld(convw, convw_d, w=["convw"])
        P.ld(chan, chan_d, w=["chan"])
        P.op("pool", lambda e: e.memset(summ, 0.0), writes=["summ"])
        for d in range(2):
            lamc = chan[:, :, 3 + 3 * d]
            P.actf(cs[:, :, d], lamc, AF.Exp, ["chan"], ["cs"], scale=-1.0)
            P.actf(cs[:, :, d], cs[:, :, d], AF.Ln, ["cs"], ["cs"], bias=1.0)
            P.ts("dve", cs[:, :, d], cs[:, :, d], -8.0, None, ALU.mult, None, ["cs"], ["cs"])

        wst = [P.sb("w1st%d" % i, [128, 8, 160], F32) for i in range(2)]
        wbf = [P.sb("w1bf%d" % i, [128, 8, 160], BF16) for i in range(2)]
        wbd_st = [P.sb("wbdst%d" % i, [BD, 4, BD], F32) for i in range(2)]
        wbd = [P.sb("wbd%d" % i, [BD, 4, BD], BF16) for i in range(2)]
        xe = P.sb("xe", [BD, NT + 3], F32)
        xce = P.sb("xce", [BD, CTX + 3], F32)
        u = P.sb("u", [BD, NT], F32)
        uc = P.sb("uc", [BD, CTX], F32)
        ub = P.sb("ub", [BD, NT], BF16)
        ucb = P.sb("ucb", [BD, CTX], BF16)
        sg = P.sb("sg", [BD, NT], F32)
        RR = [P.sb("rr%d" % d, [BD, NT], F32) for d in range(2)]
        II = [P.sb("ii%d" % d, [BD, NT], F32) for d in range(2)]
        AA = [P.sb("aa%d" % d, [BD, NT], F32) for d in range(2)]
        BB = [P.sb("bb%d" % d, [BD, NT], F32) for d in range(2)]
        RC = [P.sb("rc%d" % d, [BD, CTX], F32) for d in range(2)]
        IC = [P.sb("ic%d" % d, [BD, CTX], F32) for d in range(2)]
        AC = [P.sb("ac%d" % d, [BD, CTX], F32) for d in range(2)]
        BC = [P.sb("bc%d" % d, [BD, CTX], F32) for d in range(2)]
        HC = [P.sb("hc%d" % d, [BD, CTX], F32) for d in range(2)]
        cs2 = P.sb("cs2", [BD, NB, 2], F32)
        P.ts("dve", cs2, cs, 2.0, None, ALU.mult, None, ["cs"], ["cs2"])
        P.op("pool", lambda e: e.memset(xce, 0.0), writes=["xce"])

        def rev(ap, d):
            return ap[:, ::-1] if d == 1 else ap

        for h in range(NB):
            b = h % 2
            P.ld(wst[b][:, :, 0:80], wv[:, :, BD * h:BD * h + BD], w=[("w1st", b)])
            P.ld(wst[b][:, :, 80:160], wv[:, :, RW + BD * h:RW + BD * h + BD], w=[("w1st", b)])
            P.cp("act", wbf[b], wst[b], [("w1st", b)], [("w1bf", b)])
            P.ld(wbd_st[b], wbd_d[:, h, :, :], w=[("wbdst", b)])
            P.cp("pool", wbd[b], wbd_st[b], [("wbdst", b)], [("wbd", b)])
            hregs = [("hlT", t) for t in range(19)]
            for g in range(4):
                pb = g % 2
                P.mm([(P.bank(pb)[0:BD, :], wbf[b][:, k, 0:80], hlT[:, k, 512 * g:512 * g + 512], k == 0, k == 7) for k in range(8)],
                     [("w1bf", b)] + hregs, [("ps", pb)])
                P.cp("act", xe[:, 2 + 512 * g:2 + 512 * g + 512], P.bank(pb)[0:BD, :], [("ps", pb)], ["xe"])
            P.mm([(P.bank(2)[0:BD, 0:4], wbf[b][:, k, 0:80], hlT[:, k, 2048:2052], k == 0, k == 7) for k in range(8)],
                 [("w1bf", b)] + hregs, [("ps", 2)])
            P.ts("dve", xe[:, 0:2], P.bank(2)[0:BD, 0:2], self.flags[0:BD, 0:1], None, ALU.mult, None, [("ps", 2), "flags"], ["xe"])
            P.ts("dve", xe[:, NT + 2:NT + 3], P.bank(2)[0:BD, 2:3], self.flags[0:BD, 1:2], None, ALU.mult, None,
                 [("ps", 2), "flags"], ["xe"])
            P.mm([(P.bank(3)[0:BD, 0:CTX], wbf[b][:, k, 0:80], hlT[:, k, 2176:2432], k == 0, k == 7) for k in range(8)],
                 [("w1bf", b)] + hregs, [("ps", 3)])
            P.cp("act", xce[:, 2:2 + CTX], P.bank(3)[0:BD, 0:CTX], [("ps", 3)], ["xce"])
            for (src, dst, n, sreg, dreg) in ((xe, u, NT, "xe", "u"), (xce, uc, CTX, "xce", "uc")):
                P.ts("dve", dst, src[:, 0:n], convw[:, h, 0:1], chan[:, h, 0:1], ALU.mult, ALU.add, [sreg, "convw", "chan"], [dreg])
                for j in range(1, 4):
                    P.stt("dve", dst, src[:, j:j + n], convw[:, h, j:j + 1], dst, ALU.mult, ALU.add, [sreg, "convw", dreg], [dreg])
            P.cp("pool", ub, u, ["u"], ["ub"])
            P.cp("pool", ucb, uc, ["uc"], ["ucb"])
            for g in range(4):
                pb = 6 + g % 2
                P.mm([(P.bank(pb)[0:BD, :], wbf[b][:, k, 80:160], hlT[:, k, 512 * g:512 * g + 512], k == 0, k == 7) for k in range(8)],
                     [("w1bf", b)] + hregs, [("ps", pb)])
                P.actf(sg[:, 512 * g:512 * g + 512], P.bank(pb)[0:BD, :], AF.Silu, [("ps", pb)], ["sg"])
            for g in range(4):
                sl = slice(512 * g, 512 * g + 512)
                for d in range(2):
                    pb = 2 * d
                    ba = chan[:, h, 1 + 3 * d:2 + 3 * d]
                    bx = chan[:, h, 2 + 3 * d:3 + 3 * d]
                    P.mm([(P.bank(pb)[0:BD, :], wbd[b][:, 2 * d, :], ub[:, sl], True, True)], [("wbd", b), "ub"], [("ps", pb)])
                    P.mm([(P.bank(pb + 1)[0:BD, :], wbd[b][:, 2 * d + 1, :], ub[:, sl], True, True)], [("wbd", b), "ub"], [("ps", pb + 1)])
                    P.actf(RR[d][:, sl], P.bank(pb)[0:BD, :], AF.Sigmoid, [("ps", pb), "chan"], [("rr", d)], bias=ba)
                    P.actf(II[d][:, sl], P.bank(pb + 1)[0:BD, :], AF.Sigmoid, [("ps", pb + 1), "chan"], [("ii", d)], bias=bx)
            for d in range(2):
                ba = chan[:, h, 1 + 3 * d:2 + 3 * d]
                bx = chan[:, h, 2 + 3 * d:3 + 3 * d]
                P.mm([(P.bank(4)[0:BD, 0:CTX], wbd[b][:, 2 * d, :], ucb, True, True)], [("wbd", b), "ucb"], [("ps", 4)])
                P.mm([(P.bank(5)[0:BD, 0:CTX], wbd[b][:, 2 * d + 1, :], ucb, True, True)], [("wbd", b), "ucb"], [("ps", 5)])
                P.actf(RC[d], P.bank(4)[0:BD, 0:CTX], AF.Sigmoid, [("ps", 4), "chan"], [("rc", d)], bias=ba)
                P.actf(IC[d], P.bank(5)[0:BD, 0:CTX], AF.Sigmoid, [("ps", 5), "chan"], [("ic", d)], bias=bx)
            for (R_, I_, A_, B_, u_, nm, ureg) in ((RC, IC, AC, BC, uc, "c", "uc"), (RR, II, AA, BB, u, "", "u")):
                for d in range(2):
                    P.actf(A_[d], R_[d], AF.Exp, [("rr" + nm if nm == "" else "rc", d), "cs"], [("aa" + nm, d)], scale=cs[:, h, d:d + 1])
                    P.actf(B_[d], R_[d], AF.Exp, [("rr" if nm == "" else "rc", d), "cs2"], [("bb" + nm, d)], scale=cs2[:, h, d:d + 1])
                for d in range(2):
                    P.actf(B_[d], B_[d], AF.Sqrt, [("bb" + nm, d)], [("bb" + nm, d)], scale=-1.0, bias=1.0)
                for d in range(2):
                    eng = "dve"
                    P.tt(eng, B_[d], B_[d], I_[d], ALU.mult, [("bb" + nm, d), ("ii" if nm == "" else "ic", d)], [("bb" + nm, d)])
                    P.tt(eng, B_[d], B_[d], u_, ALU.mult, [("bb" + nm, d), ureg], [("bb" + nm, d)])
            for d in range(2):
                P.op("dve", lambda e, d=d: e.tensor_tensor_scan(out=rev(HC[d], d), data0=rev(AC[d], d), data1=rev(BC[d], d), initial=0.0,
                                                                 op0=ALU.mult, op1=ALU.add), [("aac", d), ("bbc", d)], [("hc", d)])
                last_c = CTX - 1 if d == 0 else 0
                P.cp("pool", summ[:, h, 3 * d + 2:3 * d + 3], HC[d][:, last_c:last_c + 1], [("hc", d)], ["summ"])
            for d in range(2):
                P.op("dve", lambda e, d=d: e.tensor_tensor_scan(out=rev(RR[d], d), data0=rev(AA[d], d), data1=rev(BB[d], d), initial=0.0,
                                                                 op0=ALU.mult, op1=ALU.add), [("aa", d), ("bb", d)], [("rr", d)])
                P.op("dve", lambda e, d=d: e.tensor_tensor_scan(out=rev(II[d], d), data0=rev(AA[d], d), data1=rev(AA[d], d), initial=1.0,
                                                                 op0=ALU.mult, op1=ALU.min), [("aa", d)], [("ii", d)])
                last = NT - 1 if d == 0 else 0
                P.cp("pool", summ[:, h, 3 * d:3 * d + 1], II[d][:, last:last + 1], [("ii", d)], ["summ"])
                P.cp("pool", summ[:, h, 3 * d + 1:3 * d + 2], RR[d][:, last:last + 1], [("rr", d)], ["summ"])
            P.tt("dve", RR[0], RR[0], RR[1], ALU.add, [("rr", 0), ("rr", 1)], [("rr", 0)])
            P.tt("dve", II[0], II[0], sg, ALU.mult, [("ii", 0), "sg"], [("ii", 0)])
            P.tt("dve", II[1], II[1], sg, ALU.mult, [("ii", 1), "sg"], [("ii", 1)])
            P.tt("dve", RR[0], RR[0], sg, ALU.mult, [("rr", 0), "sg"], [("rr", 0)])
            for j, (T, reg) in enumerate(((RR[0], ("rr", 0)), (II[0], ("ii", 0)), (II[1], ("ii", 1)))):
                P.dma("sp", lambda e, T=T, h=h, j=j: e.dma_start(out=spill_d[h, j], in_=T), reads=[reg])
        P.dma("sp", lambda e: e.dma_start(out=summ_d, in_=summ.rearrange("p a b -> p (a b)")), reads=["summ"])

    def l1_final(self):
        P = self.P
        if self.stage == "F":
            xl1_d = self._scratch["xl1"]
            spill_d = self._scratch["spill"]
            P.barrier()
            summ_all = self.dint("summ_all", [NCORES * BD, NB * 6], F32)
            P.cc(self._scratch["summ"], summ_all, writes=["summ_all"])
            summ_d = summ_all.rearrange("(r p) n -> r p n", p=BD)
        else:
            xl1_d = self.din("xl1_in", [NT, D])
            spill_d = self.din("spill_in", [NB, 3, BD, NT])
            summ_d = self.din("summ_all", [NCORES, BD, NB * 6])
        sel_d = self.din("selmask", [128, 16])
        w_out_d = self.din("w_out1", [RW, D])
        fg_d = self.din("final_g", [1, D])
        out_d = self.dout("out", [NT, D])

        P.sb_off = self.ZA
        if self.stage != "F":
            self.adaln(1)
        P.barrier()
        P.sb_off = self.ZA
        sa = P.sb("summ_all", [BD, NCORES, NB * 6], F32)
        sel = P.sb("sel", [128, 16], F32)
        hs = P.sb("hs", [BD, 2, NB], F32)
        tt_ = P.sb("foldt", [BD, NB], F32)
        P.ld(sa, summ_d.rearrange("r p n -> p r n"), w=["sa"], r=["summ_all"])
        P.ld(sel, sel_d, w=["sel"])
        sav = sa.rearrange("p r (h s) -> p r h s", s=6)
        for d in range(2):
            hd = hs[:, d, :]
            P.cp("dve", hd, sav[:, 0, :, 3 * d + 2], ["sa"], ["hs"])
            order = range(NCORES) if d == 0 else range(NCORES - 1, -1, -1)
            for j in order:
                A = sav[:, j, :, 3 * d]
                B = sav[:, j, :, 3 * d + 1]
                P.tt("dve", tt_, A, hd, ALU.mult, ["sa", "hs"], ["foldt"])
                P.tt("dve", tt_, tt_, B, ALU.add, ["foldt", "sa"], ["foldt"])
                P.tt("dve", tt_, tt_, hd, ALU.subtract, ["foldt", "hs"], ["foldt"])
                P.stt("dve", hd, tt_, sel[0:BD, 8 * d + j:8 * d + j + 1], hd, ALU.mult, ALU.add, ["foldt", "sel", "hs"], ["hs"])

        W2 = P.sb("W2", [BD, NB, D], BF16)
        w2st = P.sb("w2st", [BD, 2, D], F32)
        w2v = w_out_d.rearrange("(h p) n -> p h n", p=BD)
        for h2 in range(NB // 2):
            P.ld(w2st, w2v[:, 2 * h2:2 * h2 + 2, :], w=["w2st"])
            P.cp("pool", W2[:, 2 * h2:2 * h2 + 2, :], w2st, ["w2st"], [("W2", h2)])
        Z = P.sb("Z", [BD, NB, NT], BF16)
        sp = [[P.sb("sp%d_%d" % (i, j), [BD, NT], F32) for j in range(3)] for i in range(2)]
        for h in range(NB):
            b = h % 2
            for j in range(3):
                P.ld(sp[b][j], spill_d[h, j], w=[("sp", b, j)])
            P.stt("dve", sp[b][0], sp[b][1], hs[:, 0, h:h + 1], sp[b][0], ALU.mult, ALU.add,
                  [("sp", b, 0), ("sp", b, 1), "hs"], [("sp", b, 0)])
            P.stt("dve", Z[:, h, :], sp[b][2], hs[:, 1, h:h + 1], sp[b][0], ALU.mult, ALU.add,
                  [("sp", b, 0), ("sp", b, 2), "hs"], [("Z", h)])
        fg = P.sb("fg", [128, D], F32)
        P.ld(fg, fg_d.to_broadcast([128, D]), w=["fg"])
        xs = [P.sb("xs3_%d" % i, [128, D], F32) for i in range(2)]
        junk = P.sb("junk3", [128, D], BF16)
        f1 = P.sb("f1d", [128, 512], F32)
        ssd = P.sb("ssd", [128, 32], F32)
        for t in range(NTILE):
            b = t % 2
            P.ld(xs[b], xl1_d[t * 128:(t + 1) * 128, :], w=[("xs3", b)])
            for half in range(2):
                pb = 2 * b + half
                P.mm([(P.bank(pb), Z[:, h, t * 128:(t + 1) * 128], W2[:, h, half * 512:(half + 1) * 512], h == 0, h == NB - 1)
                      for h in range(NB)], [("Z", h) for h in range(NB)] + [("W2", h2) for h2 in range(NB // 2)], [("ps", pb)])
                P.tt("dve", f1, P.bank(pb), self.vec["gt"][:, half * 512:(half + 1) * 512], ALU.mult,
                     [("ps", pb), ("vec", "gt", half)], ["f1d"])
                P.tt("dve", xs[b][:, half * 512:(half + 1) * 512], xs[b][:, half * 512:(half + 1) * 512], f1, ALU.add,
                     ["f1d", ("xs3", b)], [("xs3", b)])
            P.actf(junk, xs[b], AF.Square, [("xs3", b)], ["junk3", "ssd"], accum=ssd[:, t:t + 1])
            P.actf(junk[:, 0:8], xs[b][:, 0:8], AF.Square, [("xs3", b)], ["junk3", "ssd"], accum=ssd[:, 31:32])
            P.ts("dve", ssd[:, t:t + 1], ssd[:, t:t + 1], 1.0 / D, EPS, ALU.mult, ALU.add, ["ssd"], ["ssd"])
            P.actf(ssd[:, t:t + 1], ssd[:, t:t + 1], AF.Ln, ["ssd"], ["ssd"])
            P.actf(ssd[:, t:t + 1], ssd[:, t:t + 1], AF.Exp, ["ssd"], ["ssd"], scale=-0.5)
            P.stt("dve", xs[b], xs[b], ssd[:, t:t + 1], fg, ALU.mult, ALU.mult, [("xs3", b), "ssd", "fg"], [("xs3", b)])
            P.dma("sp", lambda e, t=t, b=b: e.dma_start(out=out_d[t * 128:(t + 1) * 128, :], in_=xs[b]), reads=[("xs3", b)])

    def finish(self):
        self.P.wait_all("sp")
        self.P.emit()
        return self.nc


def rope_tables():
    nf = 16
    inv = (np.float32(10000.0) ** (-np.arange(nf, dtype=np.float32) / np.float32(nf))).astype(np.float32)
    g = np.arange(SEQ)
    row = (g // 64).astype(np.float32)
    col = (g % 64).astype(np.float32)
    ar = row[:, None] * inv
    ac = col[:, None] * inv
    ang = np.concatenate([ar, ar, ac, ac], axis=-1).astype(np.float32)
    cos = np.cos(ang).astype(np.float32)
    sin = np.sin(ang).astype(np.float32)
    sign = np.ones(64, np.float32)
    sign[0:16] = -1
    sign[32:48] = -1
    return cos, sin * sign


def core_inputs_l0(r, x, c, ctx, c_ctx, norm_g, ada_w, ada_b, attn_w_in, cosf, sinf):
    s, e = r * NT, (r + 1) * NT
    xh = np.zeros((256, D), np.float32)
    if r > 0:
        xh[0:128] = x[0, s - 128:s]
    if r < NCORES - 1:
        xh[128:256] = x[0, e:e + 128]
    cc = np.stack([c[0], c_ctx], axis=-1)
    ccT = cc.reshape(8, 128, 2).transpose(1, 0, 2).reshape(128, 16)
    lo, hi = s - 128, e + 128
    idx = np.clip(np.arange(lo, hi), 0, SEQ - 1)
    ct = np.concatenate([cosf[idx].T, cosf[idx].T], axis=0)
    sn = np.concatenate([sinf[idx].T, sinf[idx].T], axis=0)
    flags = np.zeros((128, 4), np.float32)
    flags[:, 0] = 1.0 if r > 0 else 0.0
    flags[:, 1] = 1.0 if r < NCORES - 1 else 0.0
    return {
        "ident": np.eye(128, dtype=np.float32), "flags": flags,
        "ada_w": ada_w, "ada_b": ada_b, "norm_g": norm_g, "ccT": np.ascontiguousarray(ccT),
        "xo": np.ascontiguousarray(x[0, s:e]), "xh": xh, "ctx": np.ascontiguousarray(ctx[0]),
        "w_in0": attn_w_in[0], "rope_cos": np.ascontiguousarray(ct), "rope_sin": np.ascontiguousarray(sn),
    }


def window_masks():
    kk = np.arange(128)[:, None]
    qq = np.arange(128)[None, :]
    m = np.ones((128, 640), np.float32)
    m[:, 0:128] = (kk >= qq)
    m[:, 256:384] = (kk <= qq)
    return m


def extra_inputs_B(attn_w_out, attn_sink, lq1, lk1, lq2, lk2, subln_g):
    sink = attn_sink[0]
    sink_bc = np.zeros((128, 4), np.float32)
    for c in range(4):
        sink_bc[0:64, c] = sink[2 * c]
        sink_bc[64:128, c] = sink[2 * c + 1]
    return {
        "w_out0": attn_w_out[0], "sink_bc": sink_bc, "sink_row": np.ascontiguousarray(sink[None, :]),
        "lamv": np.ascontiguousarray(np.concatenate([lq1[0], lk1[0], lq2[0], lk2[0]])[None, :]),
        "subln": np.ascontiguousarray(subln_g[0][:, None]), "masks": window_masks(),
    }


_cache = {}


def get_prog(stage):
    if stage not in _cache:
        b = Builder(stage)
        b.consts()
        if stage == "A":
            b.l0_phase1()
        elif stage == "B":
            b.l0_phase1()
            b.l0_phase2()
        elif stage == "C":
            b.l1_local()
        elif stage == "D":
            b.l1_final()
        elif stage == "F":
            b.l0_phase1()
            b.l0_phase2()
            b.P.barrier()
            b.l1_local()
            b.l1_final()
        _cache[stage] = (b.finish(), b)
    return _cache[stage]


def core_inputs_l1(r, xl1, xc1, c, c_ctx, norm_g, ada_w, ada_b, rec_w_in, rec_conv_w, rec_conv_b, rec_wa, rec_ba,
                   rec_wx, rec_bx, rec_lam):
    s, e = r * NT, (r + 1) * NT
    halo = np.zeros((128, D), np.float32)
    if r > 0:
        halo[0:2] = xl1[s - 2:s]
    if r < NCORES - 1:
        halo[2] = xl1[e]
    cc = np.stack([c[0], c_ctx], axis=-1)
    ccT = cc.reshape(8, 128, 2).transpose(1, 0, 2).reshape(128, 16)
    flags = np.zeros((128, 4), np.float32)
    flags[:, 0] = 1.0 if r > 0 else 0.0
    flags[:, 1] = 1.0 if r < NCORES - 1 else 0.0
    convw_t = rec_conv_w[0].reshape(4, NB, BD).transpose(2, 1, 0)
    chan = np.zeros((BD, NB, 8), np.float32)
    chan[:, :, 0] = rec_conv_b[0].reshape(NB, BD).T
    for d in range(2):
        chan[:, :, 1 + 3 * d] = rec_ba[0, d].reshape(NB, BD).T
        chan[:, :, 2 + 3 * d] = rec_bx[0, d].reshape(NB, BD).T
        chan[:, :, 3 + 3 * d] = rec_lam[0, d].reshape(NB, BD).T
    wbd = np.stack([rec_wa[0, 0], rec_wx[0, 0], rec_wa[0, 1], rec_wx[0, 1]], axis=0)
    wbd_t = wbd.transpose(2, 1, 0, 3)
    return {
        "ident": np.eye(128, dtype=np.float32), "flags": flags,
        "ada_w": ada_w, "ada_b": ada_b, "norm_g": norm_g, "ccT": np.ascontiguousarray(ccT),
        "xl1_in": np.ascontiguousarray(xl1[s:e]), "xl1_halo": halo, "xc1_in": np.ascontiguousarray(xc1),
        "w_in1": rec_w_in[0], "convw_t": np.ascontiguousarray(convw_t), "chan_t": chan,
        "wbd_t": np.ascontiguousarray(wbd_t),
    }


def run_layer1(inp, xl1, xc1):
    ncC, _ = get_prog("C")
    insC = [core_inputs_l1(r, xl1, xc1, inp["c"], inp["c_ctx"], inp["norm_g"], inp["ada_w"], inp["ada_b"], inp["rec_w_in"],
                           inp["rec_conv_w"], inp["rec_conv_b"], inp["rec_wa"], inp["rec_ba"], inp["rec_wx"], inp["rec_bx"],
                           inp["rec_lam"]) for r in range(NCORES)]
    resC = run_bass_kernel_spmd(ncC, insC, core_ids=list(range(NCORES))).results
    summ_all = np.stack([np.asarray(resC[r]["summ"]) for r in range(NCORES)], axis=0)
    ncD, _ = get_prog("D")
    insD = []
    for r in range(NCORES):
        sel = np.zeros((128, 16), np.float32)
        for j in range(NCORES):
            sel[:, j] = 1.0 if j < r else 0.0
            sel[:, 8 + j] = 1.0 if j > r else 0.0
        cc = np.stack([inp["c"][0], inp["c_ctx"]], axis=-1)
        ccT = cc.reshape(8, 128, 2).transpose(1, 0, 2).reshape(128, 16)
        insD.append({
            "ident": np.eye(128, dtype=np.float32), "flags": insC[r]["flags"],
            "ada_w": inp["ada_w"], "ada_b": inp["ada_b"], "norm_g": inp["norm_g"], "ccT": np.ascontiguousarray(ccT),
            "xl1_in": insC[r]["xl1_in"], "spill_in": np.asarray(resC[r]["spill"]), "summ_all": summ_all, "selmask": sel,
            "w_out1": inp["rec_w_out"][0], "final_g": np.ascontiguousarray(inp["final_g"][None, :]),
        })
    resD = run_bass_kernel_spmd(ncD, insD, core_ids=list(range(NCORES))).results
    return np.concatenate([np.asarray(resD[r]["out"]) for r in range(NCORES)], axis=0)


def run_layer0(inp):
    cosf, sinf = rope_tables()
    ncA, _ = get_prog("A")
    insA = [core_inputs_l0(r, inp["x"], inp["c"], inp["ctx"], inp["c_ctx"], inp["norm_g"], inp["ada_w"], inp["ada_b"],
                           inp["attn_w_in"], cosf, sinf) for r in range(NCORES)]
    resA = run_bass_kernel_spmd(ncA, insA, core_ids=list(range(NCORES))).results
    kt_full = np.concatenate([np.asarray(resA[r]["kt_sh"]) for r in range(NCORES)], axis=0)
    v_full = np.concatenate([np.asarray(resA[r]["v_sh"]) for r in range(NCORES)], axis=0)
    ncB, _ = get_prog("B")
    ex = extra_inputs_B(inp["attn_w_out"], inp["attn_sink"], inp["lam_q1"], inp["lam_k1"], inp["lam_q2"], inp["lam_k2"],
                        inp["subln_g"])
    insB = []
    for r in range(NCORES):
        d_ = dict(insA[r])
        d_.update(ex)
        d_["kt_full"] = kt_full
        d_["v_full"] = v_full
        insB.append(d_)
    resB = run_bass_kernel_spmd(ncB, insB, core_ids=list(range(NCORES))).results
    xl1 = np.concatenate([np.asarray(resB[r]["xl1"]) for r in range(NCORES)], axis=0)
    xc1 = np.asarray(resB[0]["xc1"])
    return xl1, xc1


def run_fused(inp):
    cosf, sinf = rope_tables()
    nc, _ = get_prog("F")
    ex = extra_inputs_B(inp["attn_w_out"], inp["attn_sink"], inp["lam_q1"], inp["lam_k1"], inp["lam_q2"], inp["lam_k2"],
                        inp["subln_g"])
    dummy = np.zeros((SEQ, D), np.float32)
    ins = []
    for r in range(NCORES):
        d_ = core_inputs_l0(r, inp["x"], inp["c"], inp["ctx"], inp["c_ctx"], inp["norm_g"], inp["ada_w"], inp["ada_b"],
                            inp["attn_w_in"], cosf, sinf)
        d_.update(ex)
        l1 = core_inputs_l1(r, dummy, dummy[:CTX], inp["c"], inp["c_ctx"], inp["norm_g"], inp["ada_w"], inp["ada_b"],
                            inp["rec_w_in"], inp["rec_conv_w"], inp["rec_conv_b"], inp["rec_wa"], inp["rec_ba"],
                            inp["rec_wx"], inp["rec_bx"], inp["rec_lam"])
        for k in ("w_in1", "convw_t", "chan_t", "wbd_t"):
            d_[k] = l1[k]
        selT = np.zeros((24, 128), np.float32)
        if r > 0:
            selT[3 * (r - 1) + 1, 0] = 1.0
            selT[3 * (r - 1) + 2, 1] = 1.0
        if r < NCORES - 1:
            selT[3 * (r + 1), 2] = 1.0
        d_["selT"] = selT
        sel = np.zeros((128, 16), np.float32)
        for j in range(NCORES):
            sel[:, j] = 1.0 if j < r else 0.0
            sel[:, 8 + j] = 1.0 if j > r else 0.0
        d_["selmask"] = sel
        d_["w_out1"] = inp["rec_w_out"][0]
        d_["final_g"] = np.ascontiguousarray(inp["final_g"][None, :])
        ins.append(d_)
    res = run_bass_kernel_spmd(nc, ins, core_ids=list(range(NCORES))).results
    return np.concatenate([np.asarray(res[r]["out"]) for r in range(NCORES)], axis=0)


def kernel(**inputs):
    inp = {k: np.ascontiguousarray(np.asarray(v, dtype=np.float32)) for k, v in inputs.items()}
    out = run_fused(inp)
    return out[None].astype(np.float32)
```
